# Optimizing a Trainium2 kernel written in Bass

```python
import jax, jax.numpy as jnp
from jax import lax
import numpy as np

D_MODEL = 2048
BATCH = 4
SEQ = 2048
DEPTH = 4
DEC_BATCH = 128
DEC_SEQ = 1
PAST_LEN = 16384
PAGE_SIZE = 128

N_META = 16
N_MIXERS = 2
N_POOL_LAYERS = (DEPTH + 1) // 2
N_RET_LAYERS = DEPTH // 2
POOL_WINDOWS = (2, 4, 8, 16)
POOL_GROUPS = len(POOL_WINDOWS)
POOL_GROUP_DIM = D_MODEL // POOL_GROUPS
POOL_BUF = max(POOL_WINDOWS) - 1
RET_HEADS = 8
RET_DK = D_MODEL // RET_HEADS
RET_DV = 2 * RET_DK
RET_VDIM = RET_HEADS * RET_DV
RET_IN = 2 * D_MODEL + 2 * RET_VDIM
RET_CHUNK = 128
ROPE_BASE = 10000.0
D_FF = 4 * D_MODEL
EPS = 1e-6

kernel_name = "pool_retention_hybrid_step"


def rmsnorm(x, g):
    xf = x.astype(jnp.float32)
    y = xf * lax.rsqrt(jnp.mean(xf * xf, axis=-1, keepdims=True) + EPS)
    return y.astype(x.dtype) * g


def head_norm(o):
    of = o.astype(jnp.float32)
    mu = jnp.mean(of, axis=-1, keepdims=True)
    var = jnp.mean(jnp.square(of - mu), axis=-1, keepdims=True)
    return ((of - mu) * lax.rsqrt(var + EPS)).astype(o.dtype)


def rope(x, pos):
    half = x.shape[-1] // 2
    inv = ROPE_BASE ** (-jnp.arange(half, dtype=jnp.float32) / half)
    ang = pos.astype(jnp.float32)[:, None] * inv[None, :]
    cos = jnp.cos(ang)[None, :, None, :].astype(x.dtype)
    sin = jnp.sin(ang)[None, :, None, :].astype(x.dtype)
    x1, x2 = x[..., :half], x[..., half:]
    return jnp.concatenate([x1 * cos - x2 * sin, x1 * sin + x2 * cos], axis=-1)


def retention_chunk(q, k, v, s, log_gamma):
    c = q.shape[1]
    idx = jnp.arange(c, dtype=jnp.float32)
    diff = idx[:, None] - idx[None, :]
    decay = jnp.where(diff[None] >= 0,
                      jnp.exp(jnp.maximum(diff, 0.0)[None] * log_gamma[:, None, None]),
                      0.0).astype(q.dtype)
    scores = jnp.einsum('bihd,bjhd->bhij', q, k) * decay[None]
    inner = jnp.einsum('bhij,bjhv->bihv', scores, v)
    q_dec = jnp.exp((idx + 1.0)[:, None] * log_gamma[None, :]).astype(q.dtype)
    cross = jnp.einsum('bihd,bhdv->bihv', q, s).astype(q.dtype) * q_dec[None, :, :, None]
    k_dec = jnp.exp((c - 1.0 - idx)[:, None] * log_gamma[None, :]).astype(k.dtype)
    s_dec = jnp.exp(c * log_gamma).astype(s.dtype)
    s_new = s * s_dec[None, :, None, None] + jnp.einsum(
        'bjhd,bjhv->bhdv', k * k_dec[None, :, :, None], v).astype(s.dtype)
    return inner + cross, s_new


def pool_layer(x, prev, pos0, norm_g, w_groups, scale):
    b, t, _ = x.shape
    u = rmsnorm(x, norm_g)
    ext = jnp.concatenate([prev.astype(u.dtype), u], axis=1)
    extf = ext.astype(jnp.float32)
    cs = jnp.concatenate([jnp.zeros((b, 1, D_MODEL), jnp.float32),
                          jnp.cumsum(extf, axis=1)], axis=1)
    end = cs[:, POOL_BUF + 1:]
    pos = pos0 + jnp.arange(t)
    outs = []
    for gi, w in enumerate(POOL_WINDOWS):
        sl = slice(gi * POOL_GROUP_DIM, (gi + 1) * POOL_GROUP_DIM)
        start = cs[:, POOL_BUF + 1 - w:POOL_BUF + 1 - w + t, sl]
        cnt = jnp.minimum(w, pos + 1).astype(jnp.float32)[None, :, None]
        outs.append((end[..., sl] - start) / cnt - extf[:, POOL_BUF:, sl])
    d = jnp.stack(outs, axis=2).astype(x.dtype)
    y = jnp.einsum('btgc,gcd->btgd', d, w_groups).reshape(b, t, D_MODEL) * scale
    return x + y, ext[:, -POOL_BUF:]


def retention_layer(x, pos, s, norm_g, w_in, w_out, log_gamma, n_lead):
    b, t, _ = x.shape
    u = rmsnorm(x, norm_g)
    qkvg = u @ w_in
    q = rope(qkvg[..., :D_MODEL].reshape(b, t, RET_HEADS, RET_DK), pos)
    k = rope(qkvg[..., D_MODEL:2 * D_MODEL].reshape(b, t, RET_HEADS, RET_DK), pos) * (RET_DK ** -0.5)
    v = qkvg[..., 2 * D_MODEL:2 * D_MODEL + RET_VDIM].reshape(b, t, RET_HEADS, RET_DV)
    g = qkvg[..., 2 * D_MODEL + RET_VDIM:]
    if n_lead > 0:
        o_lead, s = retention_chunk(q[:, :n_lead], k[:, :n_lead], v[:, :n_lead], s, log_gamma)
        n_chunks = (t - n_lead) // RET_CHUNK
        def to_chunks(a):
            return jnp.moveaxis(a[:, n_lead:].reshape(b, n_chunks, RET_CHUNK, RET_HEADS, a.shape[-1]), 1, 0)

        def step(carry, inp):
            qc, kc, vc = inp
            o_c, carry = retention_chunk(qc, kc, vc, carry, log_gamma)
            return carry, o_c
        s, o_r = lax.scan(step, s, (to_chunks(q), to_chunks(k), to_chunks(v)))
        o_r = jnp.moveaxis(o_r, 0, 1).reshape(b, t - n_lead, RET_HEADS, RET_DV)
        o = jnp.concatenate([o_lead, o_r], axis=1)
    else:
        o, s = retention_chunk(q, k, v, s, log_gamma)
    y = head_norm(o).reshape(b, t, RET_VDIM) * jax.nn.silu(g)
    return x + y @ w_out, s


def mlp_layer(x, norm_g, w_up, w_down):
    h = jnp.square(jax.nn.relu(rmsnorm(x, norm_g) @ w_up))
    return x + h @ w_down


def setup_inputs(seed: int = 0) -> dict:
    key = jax.random.key(seed)
    ks = jax.random.split(key, 16)
    f32 = jnp.float32
    nrm = lambda k, shp, sc: jax.random.normal(k, shp, f32) * sc
    return {
        "x_prompt": nrm(ks[0], (BATCH, SEQ, D_MODEL), 1.0),
        "x_sample": nrm(ks[1], (DEC_BATCH, DEC_SEQ, D_MODEL), 1.0),
        "state_pool": nrm(ks[2], (N_POOL_LAYERS, DEC_BATCH, POOL_BUF, D_MODEL), 1.0),
        "state_ret": nrm(ks[3], (N_RET_LAYERS, DEC_BATCH, RET_HEADS, RET_DK, RET_DV), 0.5),
        "meta_tokens": nrm(ks[4], (N_META, D_MODEL), 1.0),
        "pool_norm": 1.0 + nrm(ks[5], (N_POOL_LAYERS, D_MODEL), 0.02),
        "pool_w": nrm(ks[6], (N_POOL_LAYERS, POOL_GROUPS, POOL_GROUP_DIM, POOL_GROUP_DIM), POOL_GROUP_DIM ** -0.5),
        "pool_scale": 1.0 + nrm(ks[7], (N_POOL_LAYERS, D_MODEL), 0.02),
        "ret_norm": 1.0 + nrm(ks[8], (N_RET_LAYERS, D_MODEL), 0.02),
        "ret_w_in": nrm(ks[9], (N_RET_LAYERS, D_MODEL, RET_IN), D_MODEL ** -0.5),
        "ret_w_out": nrm(ks[10], (N_RET_LAYERS, RET_VDIM, D_MODEL), RET_VDIM ** -0.5),
        "mlp_norm": 1.0 + nrm(ks[11], (DEPTH, D_MODEL), 0.02),
        "mlp_w_up": nrm(ks[12], (DEPTH, D_MODEL, D_FF), D_MODEL ** -0.5),
        "mlp_w_down": nrm(ks[13], (DEPTH, D_FF, D_MODEL), D_FF ** -0.5),
        "final_norm": 1.0 + nrm(ks[14], (D_MODEL,), 0.02),
    }


def reference(x_prompt, x_sample, state_pool, state_ret, meta_tokens, pool_norm, pool_w, pool_scale,
              ret_norm, ret_w_in, ret_w_out, mlp_norm, mlp_w_up, mlp_w_down, final_norm):
    b = x_prompt.shape[0]
    db, ds = x_sample.shape[0], x_sample.shape[1]
    xp = jnp.concatenate([jnp.broadcast_to(meta_tokens[None].astype(x_prompt.dtype), (b, N_META, D_MODEL)),
                          x_prompt], axis=1)
    xs = x_sample
    pos_p = jnp.arange(xp.shape[1])
    pos_s = PAST_LEN + jnp.arange(ds)
    log_gamma = jnp.log1p(-jnp.exp2(-5.0 - jnp.arange(RET_HEADS, dtype=jnp.float32)))
    pool_p, pool_s, ret_p, ret_s = [], [], [], []
    for i in range(DEPTH):
        j = i // N_MIXERS
        if i % N_MIXERS == 0:
            xp, bp = pool_layer(xp, jnp.zeros((b, POOL_BUF, D_MODEL), xp.dtype), 0,
                                pool_norm[j], pool_w[j], pool_scale[j])
            xs, bs = pool_layer(xs, state_pool[j], PAST_LEN, pool_norm[j], pool_w[j], pool_scale[j])
            pool_p.append(bp)
            pool_s.append(bs)
        else:
            s0 = jnp.zeros((b, RET_HEADS, RET_DK, RET_DV), state_ret.dtype)
            xp, sp = retention_layer(xp, pos_p, s0, ret_norm[j], ret_w_in[j], ret_w_out[j], log_gamma, N_META)
            xs, ss = retention_layer(xs, pos_s, state_ret[j], ret_norm[j], ret_w_in[j], ret_w_out[j], log_gamma, 0)
            ret_p.append(sp)
            ret_s.append(ss)
        xp = mlp_layer(xp, mlp_norm[i], mlp_w_up[i], mlp_w_down[i])
        xs = mlp_layer(xs, mlp_norm[i], mlp_w_up[i], mlp_w_down[i])
    y_prompt = rmsnorm(xp, final_norm)[:, N_META:]
    y_sample = rmsnorm(xs, final_norm)
    new_pool_prompt = jnp.stack(pool_p, axis=0)
    new_pool_sample = jnp.stack(pool_s, axis=0)
    new_ret_prompt = jnp.stack(ret_p, axis=0)
    new_ret_sample = jnp.stack(ret_s, axis=0)
    return (y_prompt, y_sample, new_pool_prompt, new_pool_sample, new_ret_prompt, new_ret_sample)
```

```python
import numpy as np
from contextlib import ExitStack
import concourse.bass as bass
import concourse.mybir as mybir
from concourse.bass_utils import run_bass_kernel_spmd

F32 = mybir.dt.float32
BF16 = mybir.dt.bfloat16
ALU = mybir.AluOpType
AF = mybir.ActivationFunctionType

D = 2048
NT = 1056
KC = 16
TILES = [(0, 352), (352, 704), (704, 1056)]
TT = [(0, 16), (16, 32)] + [(32 + 128 * n, 160 + 128 * n) for n in range(8)]
NH = 8
EPS = 1e-6
RG = [[0, 1], [2, 3], [4, 5], [6, 7]]
GAMMA = [1.0 - 2.0 ** (-5 - h) for h in range(NH)]
DEPTH = 4
ENG = ["pe", "act", "dve", "pool", "sp"]


class Rec:
    def __init__(self, nc, es):
        self.nc, self.es = nc, es
        self.q = {e: [] for e in ENG}
        self.cnt = {}
        self.known = {e: {} for e in ENG}
        self.res = {}
        self.sems = {}

    def sem(self, key):
        if key not in self.sems:
            self.sems[key] = self.es.enter_context(self.nc.semaphore("s%d" % len(self.sems)))
        return self.sems[key]

    def _waits(self, eng, reads, writes):
        evs = []
        for r in reads:
            R = self.res.get(r)
            if R and R["w"]:
                evs.append(R["w"])
        for w in writes:
            R = self.res.get(w)
            if R:
                if R["w"]:
                    evs.append(R["w"])
                evs.extend(R["r"].items())
        need = {}
        for k, v in evs:
            if k == ("eng", "pe") and eng == "pe":
                continue
            if self.known[eng].get(k, 0) < v:
                need[k] = max(need.get(k, 0), v)
        for k, v in need.items():
            self.known[eng][k] = v
        return list(need.items())

    def _note(self, ev, reads, writes):
        for r in reads:
            R = self.res.setdefault(r, {"w": None, "r": {}})
            R["r"][ev[0]] = max(R["r"].get(ev[0], 0), ev[1])
        for w in writes:
            self.res[w] = {"w": ev, "r": {}}

    def op(self, eng, fn, reads=(), writes=(), mark=True):
        waits = self._waits(eng, reads, writes)
        key = ("eng", eng)
        if mark:
            self.cnt[key] = self.cnt.get(key, 0) + 1
            val = self.cnt[key]
        else:
            val = self.cnt.get(key, 0) + 1
        self._note((key, val), reads, writes)
        self.q[eng].append((waits, fn, (key, 1) if mark else None))

    def ext(self, eng, fn, semkey, reads=(), writes=(), inc=16):
        waits = self._waits(eng, reads, writes)
        key = ("dma", semkey)
        self.cnt[key] = self.cnt.get(key, 0) + inc
        self._note((key, self.cnt[key]), reads, writes)
        self.q[eng].append((waits, fn, (key, inc)))

    def barrier(self, engs=("pe", "act", "dve", "sp"), eng_only=False):
        for e in engs:
            need = [(k, v) for k, v in self.cnt.items() if k != ("eng", e) and self.known[e].get(k, 0) < v
                    and (k[0] == "eng" or not eng_only)]
            for k, v in need:
                self.known[e][k] = v
            self.q[e].append((need, None, None))

    def wait_keys(self, eng, keys):
        need = [(k, self.cnt[k]) for k in keys if k in self.cnt and self.known[eng].get(k, 0) < self.cnt[k]]
        for k, v in need:
            self.known[eng][k] = v
        self.q[eng].append((need, None, None))

    def emit(self, eng, h):
        for waits, fn, mk in self.q[eng]:
            for k, v in waits:
                h.wait_ge(self.sem(k), v)
            if fn is None:
                continue
            ins = fn(h)
            if mk is not None:
                ins.then_inc(self.sem(mk[0]), mk[1])

    def final_waits(self, eng, h):
        for k, v in self.cnt.items():
            h.wait_ge(self.sem(k), v)


def build_program():
    nc = bass.Bass("TRN2", target_bir_lowering=False)
    es = ExitStack()
    R = Rec(nc, es)

    def din(name, shape, dt=F32):
        return nc.dram_tensor(name, list(shape), dt, kind="ExternalInput")

    def dout(name, shape, dt=F32):
        return nc.dram_tensor(name, list(shape), dt, kind="ExternalOutput")

    xT = din("xT", [D, NT])
    poolprevT = din("poolprevT", [2, D, 16, 15])
    poolprev = din("poolprev", [2, 16, 15, D])
    state_ret = din("state_ret", [2, 16, NH, 256, 512])
    gv_d = din("gv", [128, 11 * 16])
    cos_d = din("cosT", [128, NT])
    sin_d = din("sinT", [128, NT])
    dect_d = din("dect", [128, NH * 128])
    qdec_d = din("qdec", [128, NH * 128])
    kdt_d = din("kdt", [128, 10 * NH])
    kf_d = din("kfin", [128, 10 * NH])
    mc_d = din("mc", [128, 18])
    invc_d = din("invc", [128, 4 * 16])
    ident_d = din("ident", [128, 128], BF16)
    iq_d = din("iq", [128, 16 * 16])
    pool_w = din("pool_w", [2, 2048, 512])
    ret_w_in = din("ret_w_in", [2, D, 12288])
    ret_w_out = din("ret_w_out", [2, 4096, D])
    mlp_w_up = din("mlp_w_up", [4, D, 8192])
    mlp_w_down = din("mlp_w_down", [4, 8192, D])

    yT = dout("yT", [D, NT])
    pool_tail = dout("pool_tail", [2, D, 16])
    pool_su = dout("pool_su", [2, D, 16])
    pool_sprev = dout("pool_sprev", [2, 16, 14, D])
    ret_p = dout("ret_p", [2, NH, 256, 512])
    ret_s = dout("ret_s", [2, 16, NH, 256, 512])

    cc_pin = [nc.dram_tensor("cc_pin%d" % j, [D, 16], F32) for j in range(2)]
    cc_pout = [nc.dram_tensor("cc_pout%d" % j, [2 * D, 16], F32) for j in range(2)]
    cc_sin = [[nc.dram_tensor("cc_sin%d_%d" % (j, h), [256, 512], F32) for h in range(NH)] for j in range(2)]
    cc_sout = [[nc.dram_tensor("cc_sout%d_%d" % (j, h), [512, 512], F32) for h in range(NH)] for j in range(2)]

    ARENA = 212000
    arena = es.enter_context(nc.sbuf_tensor("arena", [128, ARENA // 4], F32))
    off = [0]

    def carve_at(o, shape, dt):
        n = int(np.prod(shape))
        sz = n * (4 if dt == F32 else 2)
        assert o % 4 == 0 and sz % 4 == 0 and o + sz <= ARENA, (o, sz)
        ap = arena[:, o // 4:(o + sz) // 4]
        if dt != F32:
            ap = ap.bitcast(dt)
        if len(shape) == 2:
            ap = ap.rearrange("p (a b) -> p a b", a=shape[0])
        elif len(shape) == 3:
            ap = ap.rearrange("p (a b c) -> p a b c", a=shape[0], b=shape[1])
        return ap

    def carve(shape, dt):
        n = int(np.prod(shape))
        sz = (n * (4 if dt == F32 else 2) + 3) // 4 * 4
        ap = carve_at(off[0], shape, dt)
        off[0] += sz
        return ap

    X = carve([KC, NT], F32)
    U = carve([KC, NT], BF16)
    W = [carve([KC, 512], BF16) for _ in range(2)]
    COS = carve([NT], F32)
    SIN = carve([NT], F32)
    DECT = carve([NH, 128], F32)
    QDEC = carve([NH, 128], F32)
    GV = carve([11, 16], F32)
    KDT = carve([10, NH], F32)
    KF = carve([10, NH], F32)
    MC = carve([18], F32)
    INVC = carve([4, 16], F32)
    IDENT = carve([128], BF16)
    IQ = carve([16, 16], F32)
    ONES = carve([128], F32)
    NEGH = carve([1], F32)
    PH0 = off[0]
    PHSZ = ARENA - PH0

    banks = [es.enter_context(nc.psum_tensor("pb%d" % i, [128, 512], F32)) for i in range(8)]
    bctr = [0]

    brot = [(0, 1, 2, 3, 4, 5, 7)]

    def nbank():
        b = brot[0][bctr[0] % len(brot[0])]
        bctr[0] += 1
        return b

    wctr = [0]
    XALL = [("X", k, t) for k in range(KC) for t in range(3)]

    def XK3(k):
        return [("X", k, t) for t in range(3)]

    def dma(q, out, in_, semkey, reads=(), writes=()):
        R.ext(q, lambda h, out=out, in_=in_: h.dma_start(out=out, in_=in_), semkey, reads, writes)

    def load_w(src_aps, slot=None):
        if slot is None:
            s = wctr[0] % 2
            wctr[0] += 1
        else:
            s = slot
        for (src, view) in src_aps:
            dma("pool", view(W[s]), src, ("w", s), writes=[("W", s)])
        return s

    def wblock(dram2d, r0, c0):
        src = dram2d[r0:r0 + 2048, c0:c0 + 512].rearrange("(k p) n -> p k n", p=128)
        return [(src, lambda w: w[:, :, :])]

    def mm(out, lhsT, rhs, start, stop, reads, writes, mark):
        R.op("pe", lambda h: h.matmul(out, lhsT, rhs, start=start, stop=stop), reads, writes, mark)

    dma("sp", X[:, :, :], xT[:, :].rearrange("(k p) t -> p k t", p=128), "x", writes=XALL)
    for name, sb, dr in [("cos", COS, cos_d), ("sin", SIN, sin_d)]:
        dma("sp", sb[:, :], dr[:, :], "c_" + name, writes=[name])
    dma("sp", DECT.rearrange("p a b -> p (a b)"), dect_d[:, :], "c_dect", writes=["dect"])
    dma("sp", QDEC.rearrange("p a b -> p (a b)"), qdec_d[:, :], "c_qdec", writes=["qdec"])
    dma("sp", GV.rearrange("p a b -> p (a b)"), gv_d[:, :], "c_gv", writes=["gv"])
    dma("sp", KDT.rearrange("p a b -> p (a b)"), kdt_d[:, :], "c_kdt", writes=["kdt"])
    dma("sp", KF.rearrange("p a b -> p (a b)"), kf_d[:, :], "c_kf", writes=["kf"])
    dma("sp", MC[:, :], mc_d[:, :], "c_mc", writes=["mc"])
    dma("sp", INVC.rearrange("p a b -> p (a b)"), invc_d[:, :], "c_invc", writes=["invc"])
    dma("sp", IDENT[:, :], ident_d[:, :], "c_ident", writes=["ident"])
    dma("sp", IQ.rearrange("p a b -> p (a b)"), iq_d[:, :], "c_iq", writes=["iq"])
    R.op("dve", lambda h: h.memset(ONES[:, :], 1.0), writes=["ones"])
    R.op("dve", lambda h: h.memset(NEGH[:, :], -0.5), writes=["negh"])

    def norm_stats(RSTD, SQ):
        for ti, (a, b) in enumerate(TILES):
            bk = nbank()
            for k in range(KC):
                sq = SQ[k % 2]
                R.op("act", lambda h, sq=sq, k=k, a=a, b=b: h.activation(sq[:, 0:b - a], X[:, k, a:b], AF.Square),
                     reads=[("X", k, ti)], writes=[("sq", k % 2)])
                mm(banks[bk][:, 0:b - a], ONES[:, :], sq[:, 0:b - a], k == 0, k == KC - 1,
                   reads=[("sq", k % 2), "ones"], writes=[("ps", bk)], mark=True)
            R.op("dve", lambda h, bk=bk, a=a, b=b: h.tensor_scalar(RSTD[:, a:b], banks[bk][:, 0:b - a], 1.0 / D, EPS, ALU.mult, ALU.add),
                 reads=[("ps", bk)], writes=[("rstd", ti)])
            R.op("act", lambda h, a=a, b=b: h.activation(RSTD[:, a:b], RSTD[:, a:b], AF.Sqrt),
                 reads=[("rstd", ti)], writes=[("rstd", ti)])
            R.op("dve", lambda h, a=a, b=b: h.reciprocal(RSTD[:, a:b], RSTD[:, a:b]),
                 reads=[("rstd", ti)], writes=[("rstd", ti)])

    def norm_to_U(gi, RSTD):
        for ti, (a, b) in enumerate(TILES):
            for k in range(KC):
                R.op("dve", lambda h, k=k, a=a, b=b: h.scalar_tensor_tensor(
                    U[:, k, a:b], X[:, k, a:b], GV[:, gi, k:k + 1], RSTD[:, a:b], ALU.mult, ALU.mult),
                    reads=[("X", k, ti), ("rstd", ti), "gv"], writes=[("U", ti)])

    def lin_fm(s, wview, ndc, nk, src, srckey, evac):
        for dc in range(ndc):
            for ti, (a, b) in enumerate(TILES):
                bk = nbank()
                for k in range(nk):
                    mm(banks[bk][:, 0:b - a], wview[:, k, dc * 128:(dc + 1) * 128], src[:, k, a:b], k == 0, k == nk - 1,
                       reads=[("W", s), (srckey, ti)], writes=[("ps", bk)], mark=(k == nk - 1))
                evac(dc, ti, a, b, bk)

    def mlp(i):
        RSTD = carve_at(PH0, [NT], F32)
        SQ = [carve_at(PH0 + 4224 + 1408 * t, [352], F32) for t in range(2)]
        H = carve_at(PH0 + 8448, [KC, NT], BF16)
        RT = [carve_at(PH0 + 8448 + 33792 + 1408 * t, [352], F32) for t in range(2)]
        rtc = [0]
        norm_stats(RSTD, SQ)
        norm_to_U(6 + i, RSTD)
        for j in range(4):
            for bb in range(4):
                s = load_w(wblock(mlp_w_up[i], 0, (4 * j + bb) * 512))

                def evac_up(dc, ti, a, b, bk, bb=bb):
                    t = rtc[0] % 2
                    rtc[0] += 1
                    R.op("act", lambda h: h.activation(RT[t][:, 0:b - a], banks[bk][:, 0:b - a], AF.Relu),
                         reads=[("ps", bk)], writes=[("rt", t)])
                    R.op("dve", lambda h: h.tensor_tensor(H[:, 4 * bb + dc, a:b], RT[t][:, 0:b - a], RT[t][:, 0:b - a], ALU.mult),
                         reads=[("rt", t)], writes=[("H", ti)])
                lin_fm(s, W[s], 4, KC, U, "U", evac_up)
            for cb in range(4):
                s = load_w(wblock(mlp_w_down[i], j * 2048, cb * 512))

                def evac_dn(dc, ti, a, b, bk, cb=cb):
                    R.op("dve", lambda h: h.tensor_tensor(X[:, 4 * cb + dc, a:b], banks[bk][:, 0:b - a], X[:, 4 * cb + dc, a:b], ALU.add),
                         reads=[("ps", bk), ("X", 4 * cb + dc, ti)], writes=[("X", 4 * cb + dc, ti)])
                lin_fm(s, W[s], 4, KC, H, "H", evac_dn)

    def pool(j):
        o = PH0
        RSTD = carve_at(o, [NT], F32); o += 4224
        SQ = [carve_at(o + 1408 * t, [352], F32) for t in range(2)]; o += 2816
        UF = carve_at(o, [KC, 367], F32); o += KC * 367 * 4
        DT = carve_at(o, [KC, 352], BF16); o += KC * 352 * 2
        WS = [carve_at(o + t * 4 * 367 * 4, [4, 367], F32) for t in range(2)]; o += 2 * 4 * 367 * 4
        UT = carve_at(PH0 + 4224, [KC, 16], F32)
        NB = carve_at(PH0 + 4224 + 1024, [KC, 16], F32)
        PV = carve_at(o, [4, 16, 15], F32); o += 4 * 16 * 15 * 4
        PS_ = carve_at(o, [4, 16], F32); o += 4 * 16 * 4
        assert o - PH0 <= PHSZ, (o - PH0, PHSZ)
        gi_n, gi_s = 0 + j, 2 + j
        norm_stats(RSTD, SQ)
        s = load_w(wblock(pool_w[j], 0, 0))
        PW = W[s]

        def u_f32(dst, c0, c1):
            for k in range(KC):
                R.op("dve", lambda h, k=k: h.scalar_tensor_tensor(
                    dst[:, k, :], X[:, k, c0:c1], GV[:, gi_n, k:k + 1], RSTD[:, c0:c1], ALU.mult, ALU.mult),
                    reads=XK3(k) + [("rstd", 0), ("rstd", 1), ("rstd", 2), "gv"], writes=["pl_tmp"])

        u_f32(UT, NT - 16, NT)
        dma("sp", pool_tail[j].rearrange("(k p) t -> p k t", p=128), UT[:, :, :], ("ptail", j), reads=["pl_tmp"])
        dma("sp", cc_pin[j].ap().rearrange("(k p) t -> p k t", p=128), UT[:, :, :], ("ccpin", j), reads=["pl_tmp"], writes=[("ccpin", j)])
        R.ext("pool", lambda h: h.collective_compute("AllGather", ALU.bypass, replica_groups=RG,
                                                      ins=[cc_pin[j].ap().opt()], outs=[cc_pout[j].ap().opt()]),
              ("ccp", j), reads=[("ccpin", j)], writes=[("ccpout", j)], inc=1)
        dma("sp", NB[:, :, :], cc_pout[j].ap()[0:D, :].rearrange("(k p) t -> p k t", p=128), ("nbld", j),
            reads=[("ccpout", j)], writes=["nb"])

        def pool_mm(rhs_of, n, xcols, rkey, xti):
            for g in range(4):
                for dc in range(4):
                    bk = nbank()
                    for kc in range(4):
                        mm(banks[bk][:, 0:n], PW[:, g * 4 + kc, dc * 128:(dc + 1) * 128], rhs_of(g * 4 + kc), kc == 0, kc == 3,
                           reads=[("W", s), rkey], writes=[("ps", bk)], mark=(kc == 3))
                    ch = g * 4 + dc
                    R.op("dve", lambda h, bk=bk, ch=ch: h.scalar_tensor_tensor(
                        X[:, ch, xcols[0]:xcols[1]], banks[bk][:, 0:n], GV[:, gi_s, ch:ch + 1], X[:, ch, xcols[0]:xcols[1]], ALU.mult, ALU.add),
                        reads=[("ps", bk), "gv"] + XK3(ch), writes=[("X", ch, xti)])

        def pool_tile(ti):
            a, b = TILES[ti]
            if ti == 0:
                a = 16
            n = b - a
            for k in range(KC):
                if ti == 0:
                    R.op("dve", lambda h, k=k: h.memset(UF[:, k, 0:15], 0.0), writes=["UF"])
                else:
                    R.op("dve", lambda h, k=k, a=a: h.scalar_tensor_tensor(
                        UF[:, k, 0:15], X[:, k, a - 15:a], GV[:, gi_n, k:k + 1], RSTD[:, a - 15:a], ALU.mult, ALU.mult),
                        reads=XK3(k) + [("rstd", 0), ("rstd", 1), ("rstd", 2), "gv"], writes=["UF"])
                R.op("dve", lambda h, k=k, a=a, b=b, n=n: h.scalar_tensor_tensor(
                    UF[:, k, 15:15 + n], X[:, k, a:b], GV[:, gi_n, k:k + 1], RSTD[:, a:b], ALU.mult, ALU.mult),
                    reads=XK3(k) + [("rstd", 0), ("rstd", 1), ("rstd", 2), "gv"], writes=["UF"])
            if ti == 0:
                R.op("dve", lambda h: h.tensor_scalar(UF[:, :, 15:31], UF[:, :, 15:31], MC[:, 1:2], None, ALU.mult),
                     reads=["UF", "mc"], writes=["UF"])
                R.op("dve", lambda h: h.scalar_tensor_tensor(UF[:, :, 15:31], NB[:, :, :], MC[:, 0:1], UF[:, :, 15:31], ALU.mult, ALU.add),
                     reads=["UF", "nb", "mc"], writes=["UF"])
            L = 15 + n
            for g in range(4):
                w = 2 << g
                cur = UF[:, 4 * g:4 * g + 4, :]
                sh = 1
                for step in range(g + 1):
                    dst = WS[step % 2]
                    R.op("dve", lambda h, cur=cur, dst=dst, sh=sh: h.tensor_tensor(
                        dst[:, :, sh:L], cur[:, :, sh:L], cur[:, :, 0:L - sh], ALU.add),
                        reads=["UF", "WS"], writes=["WS"])
                    cur = dst
                    sh *= 2
                R.op("dve", lambda h, cur=cur, g=g, w=w: h.scalar_tensor_tensor(
                    DT[:, 4 * g:4 * g + 4, 0:n], cur[:, :, 15:L], 1.0 / w, UF[:, 4 * g:4 * g + 4, 15:L], ALU.mult, ALU.subtract),
                    reads=["UF", "WS"], writes=["DT"])
                if ti == 0:
                    R.op("dve", lambda h, cur=cur, g=g: h.tensor_tensor(
                        cur[:, :, 15:31], cur[:, :, 15:31], INVC[:, g:g + 1, :].to_broadcast([128, 4, 16]), ALU.mult),
                        reads=["WS", "invc"], writes=["WS"])
                    R.op("dve", lambda h, cur=cur, g=g: h.tensor_tensor(
                        DT[:, 4 * g:4 * g + 4, 0:16], cur[:, :, 15:31], UF[:, 4 * g:4 * g + 4, 15:31], ALU.subtract),
                        reads=["UF", "WS", "DT"], writes=["DT"])
            pool_mm(lambda c: DT[:, c, 0:n], n, (a, b), "DT", ti)

        for ti in (2, 1, 0):
            pool_tile(ti)

        u_f32(UT, 0, 16)
        dma("sp", pool_su[j].rearrange("(k p) t -> p k t", p=128), UT[:, :, :], ("psu", j), reads=["pl_tmp"])
        dma("sp", pool_sprev[j], poolprev[j][:, 1:15, :], ("psprev", j))
        for g in range(4):
            w = 2 << g
            dma("sp", PV[:, :, :, :], poolprevT[j][512 * g:512 * (g + 1)].rearrange("(k p) b r -> p k b r", p=128), "pv", writes=["PV"])
            R.op("dve", lambda h, w=w: h.tensor_reduce(PS_[:, :, :], PV[:, :, :, 16 - w:15], mybir.AxisListType.X, ALU.add),
                 reads=["PV"], writes=["PSs"])
            R.op("dve", lambda h, g=g: h.tensor_tensor(PS_[:, :, :], PS_[:, :, :], UT[:, 4 * g:4 * g + 4, :], ALU.add),
                 reads=["PSs", "pl_tmp"], writes=["PSs"])
            R.op("dve", lambda h, g=g, w=w: h.scalar_tensor_tensor(
                DT[:, 4 * g:4 * g + 4, 0:16], PS_[:, :, :], 1.0 / w, UT[:, 4 * g:4 * g + 4, :], ALU.mult, ALU.subtract),
                reads=["PSs", "pl_tmp", "DT"], writes=["DT"])
        pool_mm(lambda c: DT[:, c, 0:16], 16, (0, 16), "DT", 0)

    def retention(j):
        o = PH0
        RSTD = carve_at(o, [NT], F32)
        SQ = [carve_at(o + 4224 + 1408 * t, [352], F32) for t in range(2)]
        QT = carve_at(o, [2, NT], BF16); o += 2 * NT * 2
        KT = carve_at(o, [2, NT], BF16); o += 2 * NT * 2
        KTOK = carve_at(o, [10, 256], BF16); o += 10 * 256 * 2
        V = carve_at(o, [10, 512], BF16); o += 10 * 512 * 2
        QR = carve_at(o, [2, NT], F32)
        YT = carve_at(o, [4, NT], BF16); o += 2 * NT * 4
        S = carve_at(o, [2, 512], F32); o += 4096
        SB = carve_at(o, [2, 512], BF16); o += 2048
        RA = carve_at(o, [2, 352], F32)
        TA = carve_at(o + 2816, [352], F32)
        TB = carve_at(o + 4224, [352], F32)
        SG = carve_at(o, [512], F32)
        ON = carve_at(o + 2048, [512], F32)
        YB = carve_at(o + 4096, [512], BF16)
        PT = carve_at(o + 5120, [128], BF16); o += 5632
        ST = carve_at(o, [16], F32); o += 64
        SS = [carve_at(o + 2048 * t, [512], F32) for t in range(4)]
        KFIN = carve_at(o, [10, 256], BF16); o += 8192
        SSB = [carve_at(o + 1024 * t, [512], BF16) for t in range(3)]; o += 3072
        QM = carve_at(o, [16, 2, 16], BF16); o += 1024
        KM = [carve_at(o + 512 * t, [256], BF16) for t in range(2)]; o += 1024
        assert o - PH0 <= PHSZ, (o - PH0, PHSZ)
        gi = 4 + j
        norm_stats(RSTD, SQ)
        norm_to_U(gi, RSTD)
        brot[0] = (0, 1, 2, 3, 4)
        win = ret_w_in[j]
        UK = [("U", t) for t in range(3)]

        def wcols(c0, n):
            return win[:, c0:c0 + n].rearrange("(k p) n -> p k n", p=128)

        def rope_tile(s, dcs, a, b, d0, d1, dkey, ti):
            n = b - a
            for ci, dc in enumerate(dcs):
                bk = nbank()
                for k in range(KC):
                    mm(banks[bk][:, 0:n], W[s][:, k, dc * 128:(dc + 1) * 128], U[:, k, a:b], k == 0, k == KC - 1,
                       reads=[("W", s), ("U", ti)], writes=[("ps", bk)], mark=(k == KC - 1))
                (lambda bk, ci: R.op("act", lambda h: h.activation(RA[:, ci, 0:n], banks[bk][:, 0:n], AF.Copy),
                                     reads=[("ps", bk)], writes=[("RA", ci)]))(bk, ci)
            cs, sn = COS[:, a:b], SIN[:, a:b]
            R.op("dve", lambda h: h.tensor_tensor(TA[:, 0:n], RA[:, 0, 0:n], cs, ALU.mult), reads=[("RA", 0), "cos"], writes=["TA"])
            R.op("dve", lambda h: h.tensor_tensor(TB[:, 0:n], RA[:, 1, 0:n], sn, ALU.mult), reads=[("RA", 1), "sin"], writes=["TB"])
            R.op("dve", lambda h: h.tensor_tensor(d0, TA[:, 0:n], TB[:, 0:n], ALU.subtract), reads=["TA", "TB"], writes=[dkey])
            R.op("dve", lambda h: h.tensor_tensor(TA[:, 0:n], RA[:, 0, 0:n], sn, ALU.mult), reads=[("RA", 0), "sin", dkey], writes=["TA"])
            R.op("dve", lambda h: h.tensor_tensor(TB[:, 0:n], RA[:, 1, 0:n], cs, ALU.mult), reads=[("RA", 1), "cos", dkey], writes=["TB"])
            R.op("dve", lambda h: h.tensor_tensor(d1, TA[:, 0:n], TB[:, 0:n], ALU.add), reads=["TA", "TB"], writes=[dkey])

        def tok_proj(s, T, bk):
            c0, c1 = TT[T]
            C = c1 - c0
            for k in range(KC):
                mm(banks[bk][0:C, :], U[:, k, c0:c1], W[s][:, k, :], k == 0, k == KC - 1,
                   reads=[("W", s)] + UK, writes=[("ps", bk)], mark=(k == KC - 1))

        def gproj(sg, T):
            C = TT[T][1] - TT[T][0]
            bk = nbank()
            tok_proj(sg, T, bk)
            R.op("act", lambda h: h.activation(SG[0:C, :], banks[bk][0:C, :], AF.Silu), reads=[("ps", bk)], writes=["SG"])

        pend = []
        rd_pend = []

        def flush_rd():
            while rd_pend:
                rd_pend.pop(0)()

        def flush():
            while pend:
                pend.pop(0)()

        def finish(T, bko):
            finish_a(T, bko)
            finish_b(T)

        def finish_a(T, bko):
            c0, c1 = TT[T]
            C = c1 - c0
            R.op("dve", lambda h: h.bn_stats(ST[0:C, 0:6], banks[bko][0:C, :]), reads=[("ps", bko)], writes=["ST"])
            R.op("dve", lambda h: h.bn_aggr(ST[0:C, 8:10], ST[0:C, 0:6]), reads=["ST"], writes=["ST"])
            R.op("dve", lambda h: h.tensor_scalar(ST[0:C, 10:11], ST[0:C, 9:10], EPS, None, ALU.add), reads=["ST"], writes=["ST2"])
            R.op("pool", lambda h: h.tensor_tensor(ST[0:C, 11:12], ST[0:C, 10:11], NEGH[0:C, 0:1], ALU.pow),
                 reads=["ST2", "negh"], writes=["ST3"])
            R.op("dve", lambda h: h.tensor_scalar(ST[0:C, 12:13], ST[0:C, 8:9], -1.0, None, ALU.mult), reads=["ST"], writes=["ST4"])
            R.op("act", lambda h: h.activation(ON[0:C, :], banks[bko][0:C, :], AF.Identity, bias=ST[0:C, 12:13]),
                 reads=[("ps", bko), "ST4"], writes=["ON"])

        def finish_b(T):
            c0, c1 = TT[T]
            C = c1 - c0
            R.op("dve", lambda h: h.scalar_tensor_tensor(YB[0:C, :], ON[0:C, :], ST[0:C, 11:12], SG[0:C, :], ALU.mult, ALU.mult),
                 reads=["ON", "ST3", "SG"], writes=["YB"])

            def post():
                bk = nbank()
                pst = banks[bk][:, 0:256].bitcast(BF16).rearrange("p (a b) -> p a b", a=4)
                for c in range(4):
                    (lambda c: R.op("pe", lambda h: h.transpose(pst[:, c, 0:C], YB[0:C, c * 128:(c + 1) * 128], IDENT[0:C, 0:C]),
                                    reads=["YB", "ident"], writes=[("ps", bk)], mark=(c == 3)))(c)
                R.op("act", lambda h: h.activation(YT[:, :, c0:c1], pst[:, :, 0:C], AF.Copy), reads=[("ps", bk)], writes=["YT"])
            pend.append(post)

        def ktrans(T, hd):
            c0, c1 = TT[T]
            C = c1 - c0
            bk = nbank()
            pst = banks[bk][:, 0:128].bitcast(BF16).rearrange("p (a b) -> p a b", a=2)
            for c in range(2):
                (lambda c: R.op("pe", lambda h: h.transpose(pst[0:C, c, :], KT[:, c, c0:c1], IDENT[:, :]),
                                reads=["KT", "ident"], writes=[("ps", bk)], mark=(c == 1)))(c)
            R.op("act", lambda h: h.activation(KTOK[0:C, T, :], pst[0:C, :, :].rearrange("p a b -> p (a b)"), AF.Copy,
                                               scale=KDT[0:C, T, hd:hd + 1]),
                 reads=[("ps", bk), "kdt"], writes=["KTOK"])
            if T >= 1:
                R.op("dve", lambda h: h.tensor_scalar(KFIN[0:C, T, :], pst[0:C, :, :].rearrange("p a b -> p (a b)"),
                                                      KF[0:C, T, hd:hd + 1], None, ALU.mult),
                     reads=[("ps", bk), "kf"], writes=["KFIN"])

        def vproj(s_v, T):
            C = TT[T][1] - TT[T][0]
            bk = nbank()
            tok_proj(s_v, T, bk)
            R.op("act", lambda h: h.activation(V[0:C, T, :], banks[bk][0:C, :], AF.Copy), reads=[("ps", bk)], writes=["V"])

        def stage1_c(c):
            bk = nbank()
            for T in range(1, 10):
                C = TT[T][1] - TT[T][0]
                mm(banks[bk][:, :], KFIN[0:C, T, c * 128:(c + 1) * 128], V[0:C, T, :], T == 1, T == 9,
                   reads=["KFIN", "V"], writes=[("ps", bk)], mark=(T == 9))
            R.op("dve", lambda h: h.tensor_copy(S[:, c, :], banks[bk][:, :]), reads=[("ps", bk)], writes=["S"])

        def unit_load(u, hd):
            b, c = divmod(u, 2)
            dma("sp", SS[u % 4][:, :], state_ret[j, b, hd, c * 128:(c + 1) * 128, :], ("ssl", u % 4), writes=[("SS", u % 4)])

        def prep_km(b):
            km = KM[b % 2]
            R.op("dve", lambda h: h.tensor_scalar(km[0:16, :], KTOK[0:16, 0, :], MC[0:16, 2 + b:3 + b], None, ALU.mult),
                 reads=["KTOK", "mc"], writes=[("KM", b % 2)])

        def unit(u, hd, gam):
            b, c = divmod(u, 2)
            sl = u % 4
            if u + 3 < 32:
                unit_load(u + 3, hd)
            km = KM[b % 2]
            if c == 0 and b + 1 < 16:
                prep_km(b + 1)
            bk = nbank()
            mm(banks[bk][:, :], km[0:16, c * 128:(c + 1) * 128], V[0:16, 0, :], True, True,
               reads=[("KM", b % 2), "V"], writes=[("ps", bk)], mark=True)
            R.op("dve", lambda h: h.scalar_tensor_tensor(SS[sl][:, :], SS[sl][:, :], gam, banks[bk][:, :], ALU.mult, ALU.add),
                 reads=[("ps", bk), ("SS", sl)], writes=[("SS", sl)])
            R.op("act", lambda h: h.activation(SSB[u % 3][:, :], SS[sl][:, :], AF.Copy), reads=[("SS", sl)], writes=[("SSB", u % 3)])
            rd_pend.append(lambda: mm(banks[6][0:16, :], QM[:, b, c, :], SSB[u % 3][:, :], u == 0, u == 31,
                                      reads=["QM", ("SSB", u % 3)], writes=[("ps", 6)], mark=True))
            dma("sp", ret_s[j, b, hd, c * 128:(c + 1) * 128, :], SS[sl][:, :], ("sst", sl), reads=[("SS", sl)])

        def chunk(T, hd, gam, s_g):
            c0, c1 = TT[T]
            C = c1 - c0
            bks = nbank()
            for c in range(2):
                mm(banks[bks][0:C, 0:C], KT[:, c, c0:c1], QT[:, c, c0:c1], c == 0, c == 1,
                   reads=["KT", "QT"], writes=[("ps", bks)], mark=(c == 1))
            R.op("dve", lambda h: h.tensor_tensor(PT[0:C, 0:C], banks[bks][0:C, 0:C], DECT[0:C, hd, 0:C], ALU.mult),
                 reads=[("ps", bks), "dect"], writes=["PT"])
            gproj(s_g, T)
            flush_rd()
            flush()
            bko = 5 if T % 2 else 7
            mm(banks[bko][0:C, :], PT[0:C, 0:C], V[0:C, T, :], True, False, reads=["PT", "V"], writes=[("ps", bko)], mark=False)
            for c in range(2):
                mm(banks[bko][0:C, :], QT[:, c, c0:c1], SB[:, c, :], False, c == 1, reads=["QT", ("SB", c)], writes=[("ps", bko)], mark=(c == 1))
            finish_a(T, bko)
            dec = 1.0 if T == 1 else gam ** 128
            for c in range(2):
                bku = nbank()
                mm(banks[bku][:, :], KTOK[0:C, T, c * 128:(c + 1) * 128], V[0:C, T, :], True, True,
                   reads=["KTOK", "V"], writes=[("ps", bku)], mark=True)
                (lambda c, bku: R.op("dve", lambda h: h.scalar_tensor_tensor(
                    S[:, c, :], S[:, c, :], dec, banks[bku][:, :], ALU.mult, ALU.add),
                    reads=[("ps", bku), ("S", c)], writes=[("S", c)]))(c, bku)
                (lambda c: R.op("act", lambda h: h.activation(SB[:, c, :], S[:, c, :], AF.Copy),
                                reads=[("S", c)], writes=[("SB", c)]))(c)
            finish_b(T)
            return bko

        def wout_group(s_o, wo, dc, ti, a, b):
            bk = nbank()
            for kc in range(4):
                mm(banks[bk][:, 0:b - a], wo[:, kc, dc * 128:(dc + 1) * 128], YT[:, kc, a:b], kc == 0, kc == 3,
                   reads=[("W", s_o), "YT"], writes=[("ps", bk)], mark=(kc == 3))
            R.op("dve", lambda h: h.tensor_tensor(X[:, dc, a:b], banks[bk][:, 0:b - a], X[:, dc, a:b], ALU.add),
                 reads=[("ps", bk), ("X", dc, ti)], writes=[("X", dc, ti)])

        SK = [("S", 0), ("S", 1)]

        def head(hd):
            gam = GAMMA[hd]
            R.barrier()
            sA, sB = hd % 2, 1 - hd % 2
            s_qk = load_w([(wcols(hd * 256, 256), lambda w: w[:, :, 0:256]),
                           (wcols(2048 + hd * 256, 256), lambda w: w[:, :, 256:512])], slot=sA)
            for ti, (a, b) in enumerate(TILES):
                rope_tile(s_qk, (2, 3), a, b, KT[:, 0, a:b], KT[:, 1, a:b], "KT", ti)
            for T in range(10):
                ktrans(T, hd)
            s_v = load_w([(wcols(4096 + hd * 512, 512), lambda w: w[:, :, :])], slot=sB)
            for T in range(10):
                vproj(s_v, T)
            s_g = load_w([(wcols(8192 + hd * 512, 512), lambda w: w[:, :, :])], slot=sB)
            for c in range(2):
                stage1_c(c)
            dma("sp", cc_sin[j][hd].ap().rearrange("(c p) v -> p c v", p=128), S[:, :, :], ("ccsin", j, hd),
                reads=["S"], writes=[("ccsin", j, hd)])
            R.ext("pool", lambda h: h.collective_compute("AllGather", ALU.bypass, replica_groups=RG,
                                                         ins=[cc_sin[j][hd].ap().opt()], outs=[cc_sout[j][hd].ap().opt()]),
                  ("ccs", j, hd), reads=[("ccsin", j, hd)], writes=[("ccsout", j, hd)], inc=1)
            for ti, (a, b) in enumerate(TILES):
                rope_tile(s_qk, (0, 1), a, b, QR[:, 0, a:b], QR[:, 1, a:b], "YT", ti)
            for c in range(2):
                (lambda c: R.op("dve", lambda h: h.tensor_tensor(
                    QT[:, c, 32:NT].rearrange("p (n i) -> p n i", i=128), QR[:, c, 32:NT].rearrange("p (n i) -> p n i", i=128),
                    QDEC[:, hd:hd + 1, :].to_broadcast([128, 8, 128]), ALU.mult), reads=["YT", "qdec"], writes=["QT"]))(c)
                (lambda c: R.op("dve", lambda h: h.tensor_tensor(QT[:, c, 16:32], QR[:, c, 16:32], QDEC[:, hd, 0:16], ALU.mult),
                                reads=["YT", "qdec"], writes=["QT"]))(c)
            R.op("dve", lambda h: h.tensor_tensor(
                QM[:, :, :, :], QR[:, :, 0:16].unsqueeze(1).to_broadcast([128, 16, 2, 16]),
                IQ[:, :, :].unsqueeze(2).to_broadcast([128, 16, 2, 16]), ALU.mult), reads=["YT", "iq"], writes=["QM"])
            R.barrier(eng_only=True)
            prep_km(0)
            for u in range(3):
                unit_load(u, hd)
            sched = [3] + [3] * 9 + [2]
            un = [0]

            def units(n):
                for _ in range(n):
                    unit(un[0], hd, gam)
                    un[0] += 1
            units(sched[0])
            dma("sp", S[:, :, :], cc_sout[j][hd].ap()[0:256, :].rearrange("(c p) v -> p c v", p=128), ("sild", 0),
                reads=[("ccsout", j, hd)], writes=["S"] + SK)
            for c in range(2):
                (lambda c: R.op("dve", lambda h: h.tensor_scalar(S[:, c, :], S[:, c, :], MC[:, 0:1], None, ALU.mult),
                                reads=["S", ("S", c), "mc"], writes=[("S", c)]))(c)
                (lambda c: R.op("act", lambda h: h.activation(SB[:, c, :], S[:, c, :], AF.Copy), reads=[("S", c)], writes=[("SB", c)]))(c)
            for T in range(1, 10):
                bko = chunk(T, hd, gam, s_g)
                units(sched[T])
                if 2 <= T <= 5:
                    kc = T - 2
                    load_w([(ret_w_out[j][hd * 512 + kc * 128:hd * 512 + (kc + 1) * 128, :],
                             (lambda kc: lambda w: w.rearrange("p a b -> p (a b)")[:, kc * 2048:(kc + 1) * 2048])(kc))], slot=sA)
            flush_rd()
            units(sched[10])
            flush_rd()
            gproj(s_g, 0)
            flush()
            finish(0, 6)
            flush()
            dma("sp", ret_p[j, hd].rearrange("(c p) v -> p c v", p=128), S[:, :, :], ("retp", 0), reads=SK)
            s_o = sA
            wo = W[sA].rearrange("p a b -> p (a b)").rearrange("p (a b) -> p a b", a=4)
            for dc in range(16):
                for ti, (a, b) in enumerate(TILES):
                    wout_group(s_o, wo, dc, ti, a, b)

        for hd in range(NH):
            head(hd)
        brot[0] = (0, 1, 2, 3, 4, 5, 7)

    def final():
        RSTD = carve_at(PH0, [NT], F32)
        SQ = [carve_at(PH0 + 4224 + 1408 * t, [352], F32) for t in range(2)]
        YO = [carve_at(PH0 + 8448 + t * KC * 352 * 4, [KC, 352], F32) for t in range(2)]
        norm_stats(RSTD, SQ)
        for ti, (a, b) in enumerate(TILES):
            yo = YO[ti % 2]
            for k in range(KC):
                R.op("dve", lambda h, k=k, a=a, b=b, yo=yo: h.scalar_tensor_tensor(
                    yo[:, k, :], X[:, k, a:b], GV[:, 10, k:k + 1], RSTD[:, a:b], ALU.mult, ALU.mult),
                    reads=[("X", k, ti), ("rstd", ti), "gv"], writes=[("YO", ti % 2)])
            dma("sp", yT[:, a:b].rearrange("(k p) t -> p k t", p=128), yo[:, :, :], ("yo", ti % 2), reads=[("YO", ti % 2)])

    import os
    nl = int(os.environ.get("MK_LAYERS", "4"))
    for i in range(nl):
        R.barrier()
        if i % 2 == 0:
            pool(i // 2)
        else:
            retention(i // 2)
        R.barrier()
        mlp(i)
    R.barrier()
    final()

    with nc.Block() as block:
        @block.tensor
        def _(h):
            R.emit("pe", h)

        @block.scalar
        def _(h):
            R.emit("act", h)

        @block.vector
        def _(h):
            R.emit("dve", h)

        @block.gpsimd
        def _(h):
            R.emit("pool", h)

        @block.sync
        def _(h):
            R.emit("sp", h)
            R.final_waits("sp", h)
    es.close()
    global _LAST_REC
    _LAST_REC = R
    return nc


def _consts(hf):
    half = 128
    inv = (10000.0 ** (-np.arange(half, dtype=np.float32) / half)).astype(np.float32)
    pos = np.zeros(NT, np.float32)
    pos[0:16] = 16384.0
    pos[16:32] = np.arange(16)
    pos[32:] = 16 + hf * 1024 + np.arange(1024)
    ang = (inv[:, None] * pos[None, :]).astype(np.float32)
    cosT = np.cos(ang.astype(np.float64)).astype(np.float32)
    sinT = np.sin(ang.astype(np.float64)).astype(np.float32)
    g = np.array(GAMMA, np.float64)
    jj = np.arange(128)
    dect = np.zeros((128, NH, 128), np.float64)
    for h in range(NH):
        m = (jj[None, :] >= jj[:, None]).astype(np.float64)
        dect[:, h, :] = m * (g[h] ** (-(jj[:, None] + 1.0))) / 16.0
    qdec = np.broadcast_to((g[:, None] ** (jj[None, :] + 1.0))[None], (128, NH, 128))
    kdt = np.zeros((128, 10, NH), np.float64)
    kf = np.zeros((128, 10, NH), np.float64)
    kdt[:, 0, :] = 1.0 / 16.0
    for h in range(NH):
        kdt[0:16, 1, h] = (g[h] ** (15.0 - np.arange(16))) / 16.0 * (1 - hf)
        kf[0:16, 1, h] = kdt[0:16, 1, h] * g[h] ** 1024.0
        for T in range(2, 10):
            kdt[:, T, h] = (g[h] ** (127.0 - jj)) / 16.0
            kf[:, T, h] = kdt[:, T, h] * g[h] ** (128.0 * (9 - T))
    mc = np.zeros((128, 18), np.float32)
    mc[:, 0] = hf
    mc[:, 1] = 1 - hf
    mc[0:16, 2:18] = np.eye(16, dtype=np.float32)
    invc = np.zeros((128, 4, 16), np.float32)
    for gi, w in enumerate((2, 4, 8, 16)):
        invc[:, gi, :] = 1.0 / np.minimum(w, np.arange(16) + 1.0)
    import ml_dtypes
    ident = np.eye(128, dtype=np.float32).astype(ml_dtypes.bfloat16)
    iq = np.broadcast_to(np.eye(16, dtype=np.float32)[None], (128, 16, 16))
    f = lambda a: np.ascontiguousarray(a.reshape(128, -1), dtype=np.float32)
    return dict(cosT=cosT, sinT=sinT, dect=f(dect), qdec=f(qdec), kdt=f(kdt), kfin=f(kf), mc=mc,
                invc=f(invc), ident=ident, iq=f(iq))


_NC = None
_LAST_REC = None


def kernel(x_prompt, x_sample, state_pool, state_ret, meta_tokens, pool_norm, pool_w, pool_scale,
           ret_norm, ret_w_in, ret_w_out, mlp_norm, mlp_w_up, mlp_w_down, final_norm):
    global _NC
    A = lambda a: np.ascontiguousarray(np.asarray(a), dtype=np.float32)
    x_prompt, x_sample, state_pool, state_ret, meta_tokens = map(A, (x_prompt, x_sample, state_pool, state_ret, meta_tokens))
    vecs = np.stack([A(pool_norm)[0], A(pool_norm)[1], A(pool_scale)[0], A(pool_scale)[1], A(ret_norm)[0], A(ret_norm)[1],
                     A(mlp_norm)[0], A(mlp_norm)[1], A(mlp_norm)[2], A(mlp_norm)[3], A(final_norm)], 0)
    gv = np.ascontiguousarray(vecs.reshape(11, 16, 128).transpose(2, 0, 1).reshape(128, 11 * 16))
    shared = dict(gv=gv, pool_w=A(pool_w).reshape(2, 2048, 512), ret_w_in=A(ret_w_in), ret_w_out=A(ret_w_out),
                  mlp_w_up=A(mlp_w_up), mlp_w_down=A(mlp_w_down))
    consts = [_consts(0), _consts(1)]
    in_maps = []
    for c in range(8):
        s, hf = c // 2, c % 2
        rows = np.concatenate([x_sample[16 * c:16 * c + 16, 0, :], meta_tokens, x_prompt[s, hf * 1024:(hf + 1) * 1024, :]], 0)
        m = dict(shared)
        m.update(consts[hf])
        m["xT"] = np.ascontiguousarray(rows.T)
        sp = state_pool[:, 16 * c:16 * c + 16]
        m["poolprev"] = np.ascontiguousarray(sp)
        m["poolprevT"] = np.ascontiguousarray(sp.transpose(0, 3, 1, 2))
        m["state_ret"] = np.ascontiguousarray(state_ret[:, 16 * c:16 * c + 16])
        in_maps.append(m)
    if _NC is None:
        _NC = build_program()
    res = run_bass_kernel_spmd(_NC, in_maps, core_ids=list(range(8))).results
    y_prompt = np.zeros((4, 2048, D), np.float32)
    y_sample = np.zeros((128, 1, D), np.float32)
    npp = np.zeros((2, 4, 15, D), np.float32)
    nps = np.zeros((2, 128, 15, D), np.float32)
    nrp = np.zeros((2, 4, NH, 256, 512), np.float32)
    nrs = np.zeros((2, 128, NH, 256, 512), np.float32)
    for c in range(8):
        s, hf = c // 2, c % 2
        r = res[c]
        yt = r["yT"].T
        y_sample[16 * c:16 * c + 16, 0, :] = yt[0:16]
        y_prompt[s, hf * 1024:(hf + 1) * 1024, :] = yt[32:]
        nps[:, 16 * c:16 * c + 16, 0:14, :] = r["pool_sprev"]
        nps[:, 16 * c:16 * c + 16, 14, :] = r["pool_su"].transpose(0, 2, 1)
        nrs[:, 16 * c:16 * c + 16] = r["ret_s"]
        if hf == 1:
            npp[:, s] = r["pool_tail"].transpose(0, 2, 1)[:, 1:16, :]
            nrp[:, s] = r["ret_p"]
    return (y_prompt, y_sample, npp, nps, nrp, nrs)
```

```python
import numpy as np
from contextlib import ExitStack
import concourse.bass as bass
import concourse.mybir as mybir
from concourse.bass_utils import run_bass_kernel_spmd

F32 = mybir.dt.float32
BF16 = mybir.dt.bfloat16
ALU = mybir.AluOpType
AF = mybir.ActivationFunctionType

D = 2048
NT = 1056
KC = 16
TILES = [(0, 352), (352, 704), (704, 1056)]
TT = [(0, 16), (16, 32)] + [(32 + 128 * n, 160 + 128 * n) for n in range(8)]
NH = 8
EPS = 1e-6
RG = [[0, 1], [2, 3], [4, 5], [6, 7]]
GAMMA = [1.0 - 2.0 ** (-5 - h) for h in range(NH)]
DEPTH = 4
ENG = ["pe", "act", "dve", "pool", "sp"]


class Rec:
    def __init__(self, nc, es):
        self.nc, self.es = nc, es
        self.q = {e: [] for e in ENG}
        self.cnt = {}
        self.known = {e: {} for e in ENG}
        self.res = {}
        self.sems = {}

    def sem(self, key):
        if key not in self.sems:
            self.sems[key] = self.es.enter_context(self.nc.semaphore("s%d" % len(self.sems)))
        return self.sems[key]

    def _waits(self, eng, reads, writes):
        evs = []
        for r in reads:
            R = self.res.get(r)
            if R and R["w"]:
                evs.append(R["w"])
        for w in writes:
            R = self.res.get(w)
            if R:
                if R["w"]:
                    evs.append(R["w"])
                evs.extend(R["r"].items())
        need = {}
        for k, v in evs:
            if k == ("eng", "pe") and eng == "pe":
                continue
            if self.known[eng].get(k, 0) < v:
                need[k] = max(need.get(k, 0), v)
        for k, v in need.items():
            self.known[eng][k] = v
        return list(need.items())

    def _note(self, ev, reads, writes):
        for r in reads:
            R = self.res.setdefault(r, {"w": None, "r": {}})
            R["r"][ev[0]] = max(R["r"].get(ev[0], 0), ev[1])
        for w in writes:
            self.res[w] = {"w": ev, "r": {}}

    def op(self, eng, fn, reads=(), writes=(), mark=True):
        waits = self._waits(eng, reads, writes)
        key = ("eng", eng)
        if mark:
            self.cnt[key] = self.cnt.get(key, 0) + 1
            val = self.cnt[key]
        else:
            val = self.cnt.get(key, 0) + 1
        self._note((key, val), reads, writes)
        self.q[eng].append((waits, fn, (key, 1) if mark else None))

    def ext(self, eng, fn, semkey, reads=(), writes=(), inc=16):
        waits = self._waits(eng, reads, writes)
        key = ("dma", semkey)
        self.cnt[key] = self.cnt.get(key, 0) + inc
        self._note((key, self.cnt[key]), reads, writes)
        self.q[eng].append((waits, fn, (key, inc)))

    def barrier(self, engs=("pe", "act", "dve", "sp"), eng_only=False):
        for e in engs:
            need = [(k, v) for k, v in self.cnt.items() if k != ("eng", e) and self.known[e].get(k, 0) < v
                    and (k[0] == "eng" or not eng_only)]
            for k, v in need:
                self.known[e][k] = v
            self.q[e].append((need, None, None))

    def wait_keys(self, eng, keys):
        need = [(k, self.cnt[k]) for k in keys if k in self.cnt and self.known[eng].get(k, 0) < self.cnt[k]]
        for k, v in need:
            self.known[eng][k] = v
        self.q[eng].append((need, None, None))

    def emit(self, eng, h):
        for waits, fn, mk in self.q[eng]:
            for k, v in waits:
                h.wait_ge(self.sem(k), v)
            if fn is None:
                continue
            ins = fn(h)
            if mk is not None:
                ins.then_inc(self.sem(mk[0]), mk[1])

    def final_waits(self, eng, h):
        for k, v in self.cnt.items():
            h.wait_ge(self.sem(k), v)


def build_program():
    nc = bass.Bass("TRN2", target_bir_lowering=False)
    es = ExitStack()
    R = Rec(nc, es)

    def din(name, shape, dt=F32):
        return nc.dram_tensor(name, list(shape), dt, kind="ExternalInput")

    def dout(name, shape, dt=F32):
        return nc.dram_tensor(name, list(shape), dt, kind="ExternalOutput")

    xT = din("xT", [D, NT])
    poolprevT = din("poolprevT", [2, D, 16, 15])
    poolprev = din("poolprev", [2, 16, 15, D])
    state_ret = din("state_ret", [2, 16, NH, 256, 512])
    gv_d = din("gv", [128, 11 * 16])
    cos_d = din("cosT", [128, NT])
    sin_d = din("sinT", [128, NT])
    dect_d = din("dect", [128, NH * 128])
    qdec_d = din("qdec", [128, NH * 128])
    kdt_d = din("kdt", [128, 10 * NH])
    kf_d = din("kfin", [128, 10 * NH])
    mc_d = din("mc", [128, 18])
    invc_d = din("invc", [128, 4 * 16])
    ident_d = din("ident", [128, 128], BF16)
    iq_d = din("iq", [128, 16 * 16])
    pool_w = din("pool_w", [2, 2048, 512])
    ret_w_in = din("ret_w_in", [2, D, 12288])
    ret_w_out = din("ret_w_out", [2, 4096, D])
    mlp_w_up = din("mlp_w_up", [4, D, 8192])
    mlp_w_down = din("mlp_w_down", [4, 8192, D])

    yT = dout("yT", [D, NT])
    pool_tail = dout("pool_tail", [2, D, 16])
    pool_su = dout("pool_su", [2, D, 16])
    pool_sprev = dout("pool_sprev", [2, 16, 14, D])
    ret_p = dout("ret_p", [2, NH, 256, 512])
    ret_s = dout("ret_s", [2, 16, NH, 256, 512])

    cc_pin = [nc.dram_tensor("cc_pin%d" % j, [D, 16], F32) for j in range(2)]
    cc_pout = [nc.dram_tensor("cc_pout%d" % j, [2 * D, 16], F32) for j in range(2)]
    cc_sin = [[nc.dram_tensor("cc_sin%d_%d" % (j, h), [256, 512], F32) for h in range(NH)] for j in range(2)]
    cc_sout = [[nc.dram_tensor("cc_sout%d_%d" % (j, h), [512, 512], F32) for h in range(NH)] for j in range(2)]

    ARENA = 212000
    arena = es.enter_context(nc.sbuf_tensor("arena", [128, ARENA // 4], F32))
    off = [0]

    def carve_at(o, shape, dt):
        n = int(np.prod(shape))
        sz = n * (4 if dt == F32 else 2)
        assert o % 4 == 0 and sz % 4 == 0 and o + sz <= ARENA, (o, sz)
        ap = arena[:, o // 4:(o + sz) // 4]
        if dt != F32:
            ap = ap.bitcast(dt)
        if len(shape) == 2:
            ap = ap.rearrange("p (a b) -> p a b", a=shape[0])
        elif len(shape) == 3:
            ap = ap.rearrange("p (a b c) -> p a b c", a=shape[0], b=shape[1])
        return ap

    def carve(shape, dt):
        n = int(np.prod(shape))
        sz = (n * (4 if dt == F32 else 2) + 3) // 4 * 4
        ap = carve_at(off[0], shape, dt)
        off[0] += sz
        return ap

    X = carve([KC, NT], F32)
    U = carve([KC, NT], BF16)
    W = [carve([KC, 512], BF16) for _ in range(2)]
    COS = carve([NT], F32)
    SIN = carve([NT], F32)
    DECT = carve([NH, 128], F32)
    QDEC = carve([NH, 128], F32)
    GV = carve([11, 16], F32)
    KDT = carve([10, NH], F32)
    KF = carve([10, NH], F32)
    MC = carve([18], F32)
    INVC = carve([4, 16], F32)
    IDENT = carve([128], BF16)
    IQ = carve([16, 16], F32)
    ONES = carve([128], F32)
    NEGH = carve([1], F32)
    PH0 = off[0]
    PHSZ = ARENA - PH0

    banks = [es.enter_context(nc.psum_tensor("pb%d" % i, [128, 512], F32)) for i in range(8)]
    bctr = [0]

    brot = [(0, 1, 2, 3, 4, 5, 7)]

    def nbank():
        b = brot[0][bctr[0] % len(brot[0])]
        bctr[0] += 1
        return b

    wctr = [0]
    XALL = [("X", k, t) for k in range(KC) for t in range(3)]

    def XK3(k):
        return [("X", k, t) for t in range(3)]

    def dma(q, out, in_, semkey, reads=(), writes=()):
        R.ext(q, lambda h, out=out, in_=in_: h.dma_start(out=out, in_=in_), semkey, reads, writes)

    def load_w(src_aps, slot=None):
        if slot is None:
            s = wctr[0] % 2
            wctr[0] += 1
        else:
            s = slot
        for (src, view) in src_aps:
            dma("pool", view(W[s]), src, ("w", s), writes=[("W", s)])
        return s

    def wblock(dram2d, r0, c0):
        src = dram2d[r0:r0 + 2048, c0:c0 + 512].rearrange("(k p) n -> p k n", p=128)
        return [(src, lambda w: w[:, :, :])]

    def mm(out, lhsT, rhs, start, stop, reads, writes, mark):
        R.op("pe", lambda h: h.matmul(out, lhsT, rhs, start=start, stop=stop), reads, writes, mark)

    dma("sp", X[:, :, :], xT[:, :].rearrange("(k p) t -> p k t", p=128), "x", writes=XALL)
    for name, sb, dr in [("cos", COS, cos_d), ("sin", SIN, sin_d)]:
        dma("sp", sb[:, :], dr[:, :], "c_" + name, writes=[name])
    dma("sp", DECT.rearrange("p a b -> p (a b)"), dect_d[:, :], "c_dect", writes=["dect"])
    dma("sp", QDEC.rearrange("p a b -> p (a b)"), qdec_d[:, :], "c_qdec", writes=["qdec"])
    dma("sp", GV.rearrange("p a b -> p (a b)"), gv_d[:, :], "c_gv", writes=["gv"])
    dma("sp", KDT.rearrange("p a b -> p (a b)"), kdt_d[:, :], "c_kdt", writes=["kdt"])
    dma("sp", KF.rearrange("p a b -> p (a b)"), kf_d[:, :], "c_kf", writes=["kf"])
    dma("sp", MC[:, :], mc_d[:, :], "c_mc", writes=["mc"])
    dma("sp", INVC.rearrange("p a b -> p (a b)"), invc_d[:, :], "c_invc", writes=["invc"])
    dma("sp", IDENT[:, :], ident_d[:, :], "c_ident", writes=["ident"])
    dma("sp", IQ.rearrange("p a b -> p (a b)"), iq_d[:, :], "c_iq", writes=["iq"])
    R.op("dve", lambda h: h.memset(ONES[:, :], 1.0), writes=["ones"])
    R.op("dve", lambda h: h.memset(NEGH[:, :], -0.5), writes=["negh"])

    def norm_stats(RSTD, SQ):
        for ti, (a, b) in enumerate(TILES):
            bk = nbank()
            for k in range(KC):
                sq = SQ[k % 2]
                R.op("act", lambda h, sq=sq, k=k, a=a, b=b: h.activation(sq[:, 0:b - a], X[:, k, a:b], AF.Square),
                     reads=[("X", k, ti)], writes=[("sq", k % 2)])
                mm(banks[bk][:, 0:b - a], ONES[:, :], sq[:, 0:b - a], k == 0, k == KC - 1,
                   reads=[("sq", k % 2), "ones"], writes=[("ps", bk)], mark=True)
            R.op("dve", lambda h, bk=bk, a=a, b=b: h.tensor_scalar(RSTD[:, a:b], banks[bk][:, 0:b - a], 1.0 / D, EPS, ALU.mult, ALU.add),
                 reads=[("ps", bk)], writes=[("rstd", ti)])
            R.op("act", lambda h, a=a, b=b: h.activation(RSTD[:, a:b], RSTD[:, a:b], AF.Sqrt),
                 reads=[("rstd", ti)], writes=[("rstd", ti)])
            R.op("dve", lambda h, a=a, b=b: h.reciprocal(RSTD[:, a:b], RSTD[:, a:b]),
                 reads=[("rstd", ti)], writes=[("rstd", ti)])

    def norm_to_U(gi, RSTD):
        for ti, (a, b) in enumerate(TILES):
            for k in range(KC):
                R.op("dve", lambda h, k=k, a=a, b=b: h.scalar_tensor_tensor(
                    U[:, k, a:b], X[:, k, a:b], GV[:, gi, k:k + 1], RSTD[:, a:b], ALU.mult, ALU.mult),
                    reads=[("X", k, ti), ("rstd", ti), "gv"], writes=[("U", ti)])

    def lin_fm(s, wview, ndc, nk, src, srckey, evac):
        for dc in range(ndc):
            for ti, (a, b) in enumerate(TILES):
                bk = nbank()
                for k in range(nk):
                    mm(banks[bk][:, 0:b - a], wview[:, k, dc * 128:(dc + 1) * 128], src[:, k, a:b], k == 0, k == nk - 1,
                       reads=[("W", s), (srckey, ti)], writes=[("ps", bk)], mark=(k == nk - 1))
                evac(dc, ti, a, b, bk)

    def mlp(i):
        RSTD = carve_at(PH0, [NT], F32)
        SQ = [carve_at(PH0 + 4224 + 1408 * t, [352], F32) for t in range(2)]
        H = carve_at(PH0 + 8448, [KC, NT], BF16)
        RT = [carve_at(PH0 + 8448 + 33792 + 1408 * t, [352], F32) for t in range(2)]
        rtc = [0]
        norm_stats(RSTD, SQ)
        norm_to_U(6 + i, RSTD)
        for j in range(4):
            for bb in range(4):
                s = load_w(wblock(mlp_w_up[i], 0, (4 * j + bb) * 512))

                def evac_up(dc, ti, a, b, bk, bb=bb):
                    t = rtc[0] % 2
                    rtc[0] += 1
                    R.op("act", lambda h: h.activation(RT[t][:, 0:b - a], banks[bk][:, 0:b - a], AF.Relu),
                         reads=[("ps", bk)], writes=[("rt", t)])
                    R.op("dve", lambda h: h.tensor_tensor(H[:, 4 * bb + dc, a:b], RT[t][:, 0:b - a], RT[t][:, 0:b - a], ALU.mult),
                         reads=[("rt", t)], writes=[("H", ti)])
                lin_fm(s, W[s], 4, KC, U, "U", evac_up)
            for cb in range(4):
                s = load_w(wblock(mlp_w_down[i], j * 2048, cb * 512))

                def evac_dn(dc, ti, a, b, bk, cb=cb):
                    R.op("dve", lambda h: h.tensor_tensor(X[:, 4 * cb + dc, a:b], banks[bk][:, 0:b - a], X[:, 4 * cb + dc, a:b], ALU.add),
                         reads=[("ps", bk), ("X", 4 * cb + dc, ti)], writes=[("X", 4 * cb + dc, ti)])
                lin_fm(s, W[s], 4, KC, H, "H", evac_dn)

    def pool(j):
        o = PH0
        RSTD = carve_at(o, [NT], F32); o += 4224
        SQ = [carve_at(o + 1408 * t, [352], F32) for t in range(2)]; o += 2816
        UF = carve_at(o, [KC, 367], F32); o += KC * 367 * 4
        DT = carve_at(o, [KC, 352], BF16); o += KC * 352 * 2
        WS = [carve_at(o + t * 4 * 367 * 4, [4, 367], F32) for t in range(2)]; o += 2 * 4 * 367 * 4
        UT = carve_at(PH0 + 4224, [KC, 16], F32)
        NB = carve_at(PH0 + 4224 + 1024, [KC, 16], F32)
        PV = carve_at(o, [4, 16, 15], F32); o += 4 * 16 * 15 * 4
        PS_ = carve_at(o, [4, 16], F32); o += 4 * 16 * 4
        assert o - PH0 <= PHSZ, (o - PH0, PHSZ)
        gi_n, gi_s = 0 + j, 2 + j
        norm_stats(RSTD, SQ)
        s = load_w(wblock(pool_w[j], 0, 0))
        PW = W[s]

        def u_f32(dst, c0, c1):
            for k in range(KC):
                R.op("dve", lambda h, k=k: h.scalar_tensor_tensor(
                    dst[:, k, :], X[:, k, c0:c1], GV[:, gi_n, k:k + 1], RSTD[:, c0:c1], ALU.mult, ALU.mult),
                    reads=XK3(k) + [("rstd", 0), ("rstd", 1), ("rstd", 2), "gv"], writes=["pl_tmp"])

        u_f32(UT, NT - 16, NT)
        dma("sp", pool_tail[j].rearrange("(k p) t -> p k t", p=128), UT[:, :, :], ("ptail", j), reads=["pl_tmp"])
        dma("sp", cc_pin[j].ap().rearrange("(k p) t -> p k t", p=128), UT[:, :, :], ("ccpin", j), reads=["pl_tmp"], writes=[("ccpin", j)])
        R.ext("pool", lambda h: h.collective_compute("AllGather", ALU.bypass, replica_groups=RG,
                                                      ins=[cc_pin[j].ap().opt()], outs=[cc_pout[j].ap().opt()]),
              ("ccp", j), reads=[("ccpin", j)], writes=[("ccpout", j)], inc=1)
        dma("sp", NB[:, :, :], cc_pout[j].ap()[0:D, :].rearrange("(k p) t -> p k t", p=128), ("nbld", j),
            reads=[("ccpout", j)], writes=["nb"])

        def pool_mm(rhs_of, n, xcols, rkey, xti):
            for g in range(4):
                for dc in range(4):
                    bk = nbank()
                    for kc in range(4):
                        mm(banks[bk][:, 0:n], PW[:, g * 4 + kc, dc * 128:(dc + 1) * 128], rhs_of(g * 4 + kc), kc == 0, kc == 3,
                           reads=[("W", s), rkey], writes=[("ps", bk)], mark=(kc == 3))
                    ch = g * 4 + dc
                    R.op("dve", lambda h, bk=bk, ch=ch: h.scalar_tensor_tensor(
                        X[:, ch, xcols[0]:xcols[1]], banks[bk][:, 0:n], GV[:, gi_s, ch:ch + 1], X[:, ch, xcols[0]:xcols[1]], ALU.mult, ALU.add),
                        reads=[("ps", bk), "gv"] + XK3(ch), writes=[("X", ch, xti)])

        def pool_tile(ti):
            a, b = TILES[ti]
            if ti == 0:
                a = 16
            n = b - a
            for k in range(KC):
                if ti == 0:
                    R.op("dve", lambda h, k=k: h.memset(UF[:, k, 0:15], 0.0), writes=["UF"])
                else:
                    R.op("dve", lambda h, k=k, a=a: h.scalar_tensor_tensor(
                        UF[:, k, 0:15], X[:, k, a - 15:a], GV[:, gi_n, k:k + 1], RSTD[:, a - 15:a], ALU.mult, ALU.mult),
                        reads=XK3(k) + [("rstd", 0), ("rstd", 1), ("rstd", 2), "gv"], writes=["UF"])
                R.op("dve", lambda h, k=k, a=a, b=b, n=n: h.scalar_tensor_tensor(
                    UF[:, k, 15:15 + n], X[:, k, a:b], GV[:, gi_n, k:k + 1], RSTD[:, a:b], ALU.mult, ALU.mult),
                    reads=XK3(k) + [("rstd", 0), ("rstd", 1), ("rstd", 2), "gv"], writes=["UF"])
            if ti == 0:
                R.op("dve", lambda h: h.tensor_scalar(UF[:, :, 15:31], UF[:, :, 15:31], MC[:, 1:2], None, ALU.mult),
                     reads=["UF", "mc"], writes=["UF"])
                R.op("dve", lambda h: h.scalar_tensor_tensor(UF[:, :, 15:31], NB[:, :, :], MC[:, 0:1], UF[:, :, 15:31], ALU.mult, ALU.add),
                     reads=["UF", "nb", "mc"], writes=["UF"])
            L = 15 + n
            for g in range(4):
                w = 2 << g
                cur = UF[:, 4 * g:4 * g + 4, :]
                sh = 1
                for step in range(g + 1):
                    dst = WS[step % 2]
                    R.op("dve", lambda h, cur=cur, dst=dst, sh=sh: h.tensor_tensor(
                        dst[:, :, sh:L], cur[:, :, sh:L], cur[:, :, 0:L - sh], ALU.add),
                        reads=["UF", "WS"], writes=["WS"])
                    cur = dst
                    sh *= 2
                R.op("dve", lambda h, cur=cur, g=g, w=w: h.scalar_tensor_tensor(
                    DT[:, 4 * g:4 * g + 4, 0:n], cur[:, :, 15:L], 1.0 / w, UF[:, 4 * g:4 * g + 4, 15:L], ALU.mult, ALU.subtract),
                    reads=["UF", "WS"], writes=["DT"])
                if ti == 0:
                    R.op("dve", lambda h, cur=cur, g=g: h.tensor_tensor(
                        cur[:, :, 15:31], cur[:, :, 15:31], INVC[:, g:g + 1, :].to_broadcast([128, 4, 16]), ALU.mult),
                        reads=["WS", "invc"], writes=["WS"])
                    R.op("dve", lambda h, cur=cur, g=g: h.tensor_tensor(
                        DT[:, 4 * g:4 * g + 4, 0:16], cur[:, :, 15:31], UF[:, 4 * g:4 * g + 4, 15:31], ALU.subtract),
                        reads=["UF", "WS", "DT"], writes=["DT"])
            pool_mm(lambda c: DT[:, c, 0:n], n, (a, b), "DT", ti)

        for ti in (2, 1, 0):
            pool_tile(ti)

        u_f32(UT, 0, 16)
        dma("sp", pool_su[j].rearrange("(k p) t -> p k t", p=128), UT[:, :, :], ("psu", j), reads=["pl_tmp"])
        dma("sp", pool_sprev[j], poolprev[j][:, 1:15, :], ("psprev", j))
        for g in range(4):
            w = 2 << g
            dma("sp", PV[:, :, :, :], poolprevT[j][512 * g:512 * (g + 1)].rearrange("(k p) b r -> p k b r", p=128), "pv", writes=["PV"])
            R.op("dve", lambda h, w=w: h.tensor_reduce(PS_[:, :, :], PV[:, :, :, 16 - w:15], mybir.AxisListType.X, ALU.add),
                 reads=["PV"], writes=["PSs"])
            R.op("dve", lambda h, g=g: h.tensor_tensor(PS_[:, :, :], PS_[:, :, :], UT[:, 4 * g:4 * g + 4, :], ALU.add),
                 reads=["PSs", "pl_tmp"], writes=["PSs"])
            R.op("dve", lambda h, g=g, w=w: h.scalar_tensor_tensor(
                DT[:, 4 * g:4 * g + 4, 0:16], PS_[:, :, :], 1.0 / w, UT[:, 4 * g:4 * g + 4, :], ALU.mult, ALU.subtract),
                reads=["PSs", "pl_tmp", "DT"], writes=["DT"])
        pool_mm(lambda c: DT[:, c, 0:16], 16, (0, 16), "DT", 0)

    def retention(j):
        o = PH0
        RSTD = carve_at(o, [NT], F32)
        SQ = [carve_at(o + 4224 + 1408 * t, [352], F32) for t in range(2)]
        QT = carve_at(o, [2, NT], BF16); o += 2 * NT * 2
        KT = carve_at(o, [2, NT], BF16); o += 2 * NT * 2
        KTOK = carve_at(o, [10, 256], BF16); o += 10 * 256 * 2
        V = carve_at(o, [10, 512], BF16); o += 10 * 512 * 2
        QR = carve_at(o, [2, NT], F32)
        YT = carve_at(o, [4, NT], BF16); o += 2 * NT * 4
        S = carve_at(o, [2, 512], F32); o += 4096
        SB = carve_at(o, [2, 512], BF16); o += 2048
        RA = carve_at(o, [2, 352], F32)
        TA = carve_at(o + 2816, [352], F32)
        TB = carve_at(o + 4224, [352], F32)
        SG = carve_at(o, [512], F32)
        ON = carve_at(o + 2048, [512], F32)
        YB = carve_at(o + 4096, [512], BF16)
        PT = carve_at(o + 5120, [128], BF16); o += 5632
        ST = carve_at(o, [16], F32); o += 64
        SS = [carve_at(o + 2048 * t, [512], F32) for t in range(4)]
        KFIN = carve_at(o, [10, 256], BF16); o += 8192
        SSB = [carve_at(o + 1024 * t, [512], BF16) for t in range(3)]; o += 3072
        QM = carve_at(o, [16, 2, 16], BF16); o += 1024
        KM = [carve_at(o + 512 * t, [256], BF16) for t in range(2)]; o += 1024
        assert o - PH0 <= PHSZ, (o - PH0, PHSZ)
        gi = 4 + j
        norm_stats(RSTD, SQ)
        norm_to_U(gi, RSTD)
        brot[0] = (0, 1, 2, 3, 4)
        win = ret_w_in[j]
        UK = [("U", t) for t in range(3)]

        def wcols(c0, n):
            return win[:, c0:c0 + n].rearrange("(k p) n -> p k n", p=128)

        def rope_tile(s, dcs, a, b, d0, d1, dkey, ti):
            n = b - a
            for ci, dc in enumerate(dcs):
                bk = nbank()
                for k in range(KC):
                    mm(banks[bk][:, 0:n], W[s][:, k, dc * 128:(dc + 1) * 128], U[:, k, a:b], k == 0, k == KC - 1,
                       reads=[("W", s), ("U", ti)], writes=[("ps", bk)], mark=(k == KC - 1))
                (lambda bk, ci: R.op("act", lambda h: h.activation(RA[:, ci, 0:n], banks[bk][:, 0:n], AF.Copy),
                                     reads=[("ps", bk)], writes=[("RA", ci)]))(bk, ci)
            cs, sn = COS[:, a:b], SIN[:, a:b]
            R.op("dve", lambda h: h.tensor_tensor(TA[:, 0:n], RA[:, 0, 0:n], cs, ALU.mult), reads=[("RA", 0), "cos"], writes=["TA"])
            R.op("dve", lambda h: h.tensor_tensor(TB[:, 0:n], RA[:, 1, 0:n], sn, ALU.mult), reads=[("RA", 1), "sin"], writes=["TB"])
            R.op("dve", lambda h: h.tensor_tensor(d0, TA[:, 0:n], TB[:, 0:n], ALU.subtract), reads=["TA", "TB"], writes=[dkey])
            R.op("dve", lambda h: h.tensor_tensor(TA[:, 0:n], RA[:, 0, 0:n], sn, ALU.mult), reads=[("RA", 0), "sin", dkey], writes=["TA"])
            R.op("dve", lambda h: h.tensor_tensor(TB[:, 0:n], RA[:, 1, 0:n], cs, ALU.mult), reads=[("RA", 1), "cos", dkey], writes=["TB"])
            R.op("dve", lambda h: h.tensor_tensor(d1, TA[:, 0:n], TB[:, 0:n], ALU.add), reads=["TA", "TB"], writes=[dkey])

        def tok_proj(s, T, bk):
            c0, c1 = TT[T]
            C = c1 - c0
            for k in range(KC):
                mm(banks[bk][0:C, :], U[:, k, c0:c1], W[s][:, k, :], k == 0, k == KC - 1,
                   reads=[("W", s)] + UK, writes=[("ps", bk)], mark=(k == KC - 1))

        def gproj(sg, T):
            C = TT[T][1] - TT[T][0]
            bk = nbank()
            tok_proj(sg, T, bk)
            R.op("act", lambda h: h.activation(SG[0:C, :], banks[bk][0:C, :], AF.Silu), reads=[("ps", bk)], writes=["SG"])

        pend = []
        rd_pend = []

        def flush_rd():
            while rd_pend:
                rd_pend.pop(0)()

        def flush():
            while pend:
                pend.pop(0)()

        def finish(T, bko):
            finish_a(T, bko)
            finish_b(T)

        def finish_a(T, bko):
            c0, c1 = TT[T]
            C = c1 - c0
            R.op("dve", lambda h: h.bn_stats(ST[0:C, 0:6], banks[bko][0:C, :]), reads=[("ps", bko)], writes=["ST"])
            R.op("dve", lambda h: h.bn_aggr(ST[0:C, 8:10], ST[0:C, 0:6]), reads=["ST"], writes=["ST"])
            R.op("dve", lambda h: h.tensor_scalar(ST[0:C, 10:11], ST[0:C, 9:10], EPS, None, ALU.add), reads=["ST"], writes=["ST2"])
            R.op("pool", lambda h: h.tensor_tensor(ST[0:C, 11:12], ST[0:C, 10:11], NEGH[0:C, 0:1], ALU.pow),
                 reads=["ST2", "negh"], writes=["ST3"])
            R.op("dve", lambda h: h.tensor_scalar(ST[0:C, 12:13], ST[0:C, 8:9], -1.0, None, ALU.mult), reads=["ST"], writes=["ST4"])
            R.op("act", lambda h: h.activation(ON[0:C, :], banks[bko][0:C, :], AF.Identity, bias=ST[0:C, 12:13]),
                 reads=[("ps", bko), "ST4"], writes=["ON"])

        def finish_b(T):
            c0, c1 = TT[T]
            C = c1 - c0
            R.op("dve", lambda h: h.scalar_tensor_tensor(YB[0:C, :], ON[0:C, :], ST[0:C, 11:12], SG[0:C, :], ALU.mult, ALU.mult),
                 reads=["ON", "ST3", "SG"], writes=["YB"])

            def post():
                bk = nbank()
                pst = banks[bk][:, 0:256].bitcast(BF16).rearrange("p (a b) -> p a b", a=4)
                for c in range(4):
                    (lambda c: R.op("pe", lambda h: h.transpose(pst[:, c, 0:C], YB[0:C, c * 128:(c + 1) * 128], IDENT[0:C, 0:C]),
                                    reads=["YB", "ident"], writes=[("ps", bk)], mark=(c == 3)))(c)
                R.op("act", lambda h: h.activation(YT[:, :, c0:c1], pst[:, :, 0:C], AF.Copy), reads=[("ps", bk)], writes=["YT"])
            pend.append(post)

        def ktrans(T, hd):
            c0, c1 = TT[T]
            C = c1 - c0
            bk = nbank()
            pst = banks[bk][:, 0:128].bitcast(BF16).rearrange("p (a b) -> p a b", a=2)
            for c in range(2):
                (lambda c: R.op("pe", lambda h: h.transpose(pst[0:C, c, :], KT[:, c, c0:c1], IDENT[:, :]),
                                reads=["KT", "ident"], writes=[("ps", bk)], mark=(c == 1)))(c)
            R.op("act", lambda h: h.activation(KTOK[0:C, T, :], pst[0:C, :, :].rearrange("p a b -> p (a b)"), AF.Copy,
                                               scale=KDT[0:C, T, hd:hd + 1]),
                 reads=[("ps", bk), "kdt"], writes=["KTOK"])
            if T >= 1:
                R.op("dve", lambda h: h.tensor_scalar(KFIN[0:C, T, :], pst[0:C, :, :].rearrange("p a b -> p (a b)"),
                                                      KF[0:C, T, hd:hd + 1], None, ALU.mult),
                     reads=[("ps", bk), "kf"], writes=["KFIN"])

        def vproj(s_v, T):
            C = TT[T][1] - TT[T][0]
            bk = nbank()
            tok_proj(s_v, T, bk)
            R.op("act", lambda h: h.activation(V[0:C, T, :], banks[bk][0:C, :], AF.Copy), reads=[("ps", bk)], writes=["V"])

        def stage1_c(c):
            bk = nbank()
            for T in range(1, 10):
                C = TT[T][1] - TT[T][0]
                mm(banks[bk][:, :], KFIN[0:C, T, c * 128:(c + 1) * 128], V[0:C, T, :], T == 1, T == 9,
                   reads=["KFIN", "V"], writes=[("ps", bk)], mark=(T == 9))
            R.op("dve", lambda h: h.tensor_copy(S[:, c, :], banks[bk][:, :]), reads=[("ps", bk)], writes=["S"])

        def unit_load(u, hd):
            b, c = divmod(u, 2)
            dma("sp", SS[u % 4][:, :], state_ret[j, b, hd, c * 128:(c + 1) * 128, :], ("ssl", u % 4), writes=[("SS", u % 4)])

        def prep_km(b):
            km = KM[b % 2]
            R.op("dve", lambda h: h.tensor_scalar(km[0:16, :], KTOK[0:16, 0, :], MC[0:16, 2 + b:3 + b], None, ALU.mult),
                 reads=["KTOK", "mc"], writes=[("KM", b % 2)])

        def unit(u, hd, gam):
            b, c = divmod(u, 2)
            sl = u % 4
            if u + 3 < 32:
                unit_load(u + 3, hd)
            km = KM[b % 2]
            if c == 0 and b + 1 < 16:
                prep_km(b + 1)
            bk = nbank()
            mm(banks[bk][:, :], km[0:16, c * 128:(c + 1) * 128], V[0:16, 0, :], True, True,
               reads=[("KM", b % 2), "V"], writes=[("ps", bk)], mark=True)
            R.op("dve", lambda h: h.scalar_tensor_tensor(SS[sl][:, :], SS[sl][:, :], gam, banks[bk][:, :], ALU.mult, ALU.add),
                 reads=[("ps", bk), ("SS", sl)], writes=[("SS", sl)])
            R.op("act", lambda h: h.activation(SSB[u % 3][:, :], SS[sl][:, :], AF.Copy), reads=[("SS", sl)], writes=[("SSB", u % 3)])
            rd_pend.append(lambda: mm(banks[6][0:16, :], QM[:, b, c, :], SSB[u % 3][:, :], u == 0, u == 31,
                                      reads=["QM", ("SSB", u % 3)], writes=[("ps", 6)], mark=True))
            dma("act", ret_s[j, b, hd, c * 128:(c + 1) * 128, :], SS[sl][:, :], ("sst", sl), reads=[("SS", sl)])

        def chunk(T, hd, gam, s_g):
            c0, c1 = TT[T]
            C = c1 - c0
            bks = nbank()
            for c in range(2):
                mm(banks[bks][0:C, 0:C], KT[:, c, c0:c1], QT[:, c, c0:c1], c == 0, c == 1,
                   reads=["KT", "QT"], writes=[("ps", bks)], mark=(c == 1))
            R.op("dve", lambda h: h.tensor_tensor(PT[0:C, 0:C], banks[bks][0:C, 0:C], DECT[0:C, hd, 0:C], ALU.mult),
                 reads=[("ps", bks), "dect"], writes=["PT"])
            gproj(s_g, T)
            flush_rd()
            flush()
            bko = 5 if T % 2 else 7
            mm(banks[bko][0:C, :], PT[0:C, 0:C], V[0:C, T, :], True, False, reads=["PT", "V"], writes=[("ps", bko)], mark=False)
            for c in range(2):
                mm(banks[bko][0:C, :], QT[:, c, c0:c1], SB[:, c, :], False, c == 1, reads=["QT", ("SB", c)], writes=[("ps", bko)], mark=(c == 1))
            finish_a(T, bko)
            dec = 1.0 if T == 1 else gam ** 128
            for c in range(2):
                bku = nbank()
                mm(banks[bku][:, :], KTOK[0:C, T, c * 128:(c + 1) * 128], V[0:C, T, :], True, True,
                   reads=["KTOK", "V"], writes=[("ps", bku)], mark=True)
                (lambda c, bku: R.op("dve", lambda h: h.scalar_tensor_tensor(
                    S[:, c, :], S[:, c, :], dec, banks[bku][:, :], ALU.mult, ALU.add),
                    reads=[("ps", bku), ("S", c)], writes=[("S", c)]))(c, bku)
                (lambda c: R.op("act", lambda h: h.activation(SB[:, c, :], S[:, c, :], AF.Copy),
                                reads=[("S", c)], writes=[("SB", c)]))(c)
            finish_b(T)
            return bko

        def wout_group(s_o, wo, dc, ti, a, b):
            bk = nbank()
            for kc in range(4):
                mm(banks[bk][:, 0:b - a], wo[:, kc, dc * 128:(dc + 1) * 128], YT[:, kc, a:b], kc == 0, kc == 3,
                   reads=[("W", s_o), "YT"], writes=[("ps", bk)], mark=(kc == 3))
            R.op("dve", lambda h: h.tensor_tensor(X[:, dc, a:b], banks[bk][:, 0:b - a], X[:, dc, a:b], ALU.add),
                 reads=[("ps", bk), ("X", dc, ti)], writes=[("X", dc, ti)])

        SK = [("S", 0), ("S", 1)]

        def head(hd):
            gam = GAMMA[hd]
            R.barrier()
            sA, sB = hd % 2, 1 - hd % 2
            s_qk = load_w([(wcols(hd * 256, 256), lambda w: w[:, :, 0:256]),
                           (wcols(2048 + hd * 256, 256), lambda w: w[:, :, 256:512])], slot=sA)
            for ti, (a, b) in enumerate(TILES):
                rope_tile(s_qk, (2, 3), a, b, KT[:, 0, a:b], KT[:, 1, a:b], "KT", ti)
            for T in range(10):
                ktrans(T, hd)
            s_v = load_w([(wcols(4096 + hd * 512, 512), lambda w: w[:, :, :])], slot=sB)
            for T in range(10):
                vproj(s_v, T)
            s_g = load_w([(wcols(8192 + hd * 512, 512), lambda w: w[:, :, :])], slot=sB)
            for c in range(2):
                stage1_c(c)
            dma("sp", cc_sin[j][hd].ap().rearrange("(c p) v -> p c v", p=128), S[:, :, :], ("ccsin", j, hd),
                reads=["S"], writes=[("ccsin", j, hd)])
            R.ext("pool", lambda h: h.collective_compute("AllGather", ALU.bypass, replica_groups=RG,
                                                         ins=[cc_sin[j][hd].ap().opt()], outs=[cc_sout[j][hd].ap().opt()]),
                  ("ccs", j, hd), reads=[("ccsin", j, hd)], writes=[("ccsout", j, hd)], inc=1)
            for ti, (a, b) in enumerate(TILES):
                rope_tile(s_qk, (0, 1), a, b, QR[:, 0, a:b], QR[:, 1, a:b], "YT", ti)
            for c in range(2):
                (lambda c: R.op("dve", lambda h: h.tensor_tensor(
                    QT[:, c, 32:NT].rearrange("p (n i) -> p n i", i=128), QR[:, c, 32:NT].rearrange("p (n i) -> p n i", i=128),
                    QDEC[:, hd:hd + 1, :].to_broadcast([128, 8, 128]), ALU.mult), reads=["YT", "qdec"], writes=["QT"]))(c)
                (lambda c: R.op("dve", lambda h: h.tensor_tensor(QT[:, c, 16:32], QR[:, c, 16:32], QDEC[:, hd, 0:16], ALU.mult),
                                reads=["YT", "qdec"], writes=["QT"]))(c)
            R.op("dve", lambda h: h.tensor_tensor(
                QM[:, :, :, :], QR[:, :, 0:16].unsqueeze(1).to_broadcast([128, 16, 2, 16]),
                IQ[:, :, :].unsqueeze(2).to_broadcast([128, 16, 2, 16]), ALU.mult), reads=["YT", "iq"], writes=["QM"])
            R.barrier(eng_only=True)
            prep_km(0)
            for u in range(3):
                unit_load(u, hd)
            sched = [3] + [3] * 9 + [2]
            un = [0]

            def units(n):
                for _ in range(n):
                    unit(un[0], hd, gam)
                    un[0] += 1
            units(sched[0])
            dma("sp", S[:, :, :], cc_sout[j][hd].ap()[0:256, :].rearrange("(c p) v -> p c v", p=128), ("sild", 0),
                reads=[("ccsout", j, hd)], writes=["S"] + SK)
            for c in range(2):
                (lambda c: R.op("dve", lambda h: h.tensor_scalar(S[:, c, :], S[:, c, :], MC[:, 0:1], None, ALU.mult),
                                reads=["S", ("S", c), "mc"], writes=[("S", c)]))(c)
                (lambda c: R.op("act", lambda h: h.activation(SB[:, c, :], S[:, c, :], AF.Copy), reads=[("S", c)], writes=[("SB", c)]))(c)
            for T in range(1, 10):
                bko = chunk(T, hd, gam, s_g)
                units(sched[T])
                if 2 <= T <= 5:
                    kc = T - 2
                    load_w([(ret_w_out[j][hd * 512 + kc * 128:hd * 512 + (kc + 1) * 128, :],
                             (lambda kc: lambda w: w.rearrange("p a b -> p (a b)")[:, kc * 2048:(kc + 1) * 2048])(kc))], slot=sA)
            flush_rd()
            units(sched[10])
            flush_rd()
            gproj(s_g, 0)
            flush()
            finish(0, 6)
            flush()
            dma("sp", ret_p[j, hd].rearrange("(c p) v -> p c v", p=128), S[:, :, :], ("retp", 0), reads=SK)
            s_o = sA
            wo = W[sA].rearrange("p a b -> p (a b)").rearrange("p (a b) -> p a b", a=4)
            for dc in range(16):
                for ti, (a, b) in enumerate(TILES):
                    wout_group(s_o, wo, dc, ti, a, b)

        for hd in range(NH):
            head(hd)
        brot[0] = (0, 1, 2, 3, 4, 5, 7)

    def final():
        RSTD = carve_at(PH0, [NT], F32)
        SQ = [carve_at(PH0 + 4224 + 1408 * t, [352], F32) for t in range(2)]
        YO = [carve_at(PH0 + 8448 + t * KC * 352 * 4, [KC, 352], F32) for t in range(2)]
        norm_stats(RSTD, SQ)
        for ti, (a, b) in enumerate(TILES):
            yo = YO[ti % 2]
            for k in range(KC):
                R.op("dve", lambda h, k=k, a=a, b=b, yo=yo: h.scalar_tensor_tensor(
                    yo[:, k, :], X[:, k, a:b], GV[:, 10, k:k + 1], RSTD[:, a:b], ALU.mult, ALU.mult),
                    reads=[("X", k, ti), ("rstd", ti), "gv"], writes=[("YO", ti % 2)])
            dma("sp", yT[:, a:b].rearrange("(k p) t -> p k t", p=128), yo[:, :, :], ("yo", ti % 2), reads=[("YO", ti % 2)])

    import os
    nl = int(os.environ.get("MK_LAYERS", "4"))
    for i in range(nl):
        R.barrier()
        if i % 2 == 0:
            pool(i // 2)
        else:
            retention(i // 2)
        R.barrier()
        mlp(i)
    R.barrier()
    final()

    with nc.Block() as block:
        @block.tensor
        def _(h):
            R.emit("pe", h)

        @block.scalar
        def _(h):
            R.emit("act", h)

        @block.vector
        def _(h):
            R.emit("dve", h)

        @block.gpsimd
        def _(h):
            R.emit("pool", h)

        @block.sync
        def _(h):
            R.emit("sp", h)
            R.final_waits("sp", h)
    es.close()
    global _LAST_REC
    _LAST_REC = R
    return nc


def _consts(hf):
    half = 128
    inv = (10000.0 ** (-np.arange(half, dtype=np.float32) / half)).astype(np.float32)
    pos = np.zeros(NT, np.float32)
    pos[0:16] = 16384.0
    pos[16:32] = np.arange(16)
    pos[32:] = 16 + hf * 1024 + np.arange(1024)
    ang = (inv[:, None] * pos[None, :]).astype(np.float32)
    cosT = np.cos(ang.astype(np.float64)).astype(np.float32)
    sinT = np.sin(ang.astype(np.float64)).astype(np.float32)
    g = np.array(GAMMA, np.float64)
    jj = np.arange(128)
    dect = np.zeros((128, NH, 128), np.float64)
    for h in range(NH):
        m = (jj[None, :] >= jj[:, None]).astype(np.float64)
        dect[:, h, :] = m * (g[h] ** (-(jj[:, None] + 1.0))) / 16.0
    qdec = np.broadcast_to((g[:, None] ** (jj[None, :] + 1.0))[None], (128, NH, 128))
    kdt = np.zeros((128, 10, NH), np.float64)
    kf = np.zeros((128, 10, NH), np.float64)
    kdt[:, 0, :] = 1.0 / 16.0
    for h in range(NH):
        kdt[0:16, 1, h] = (g[h] ** (15.0 - np.arange(16))) / 16.0 * (1 - hf)
        kf[0:16, 1, h] = kdt[0:16, 1, h] * g[h] ** 1024.0
        for T in range(2, 10):
            kdt[:, T, h] = (g[h] ** (127.0 - jj)) / 16.0
            kf[:, T, h] = kdt[:, T, h] * g[h] ** (128.0 * (9 - T))
    mc = np.zeros((128, 18), np.float32)
    mc[:, 0] = hf
    mc[:, 1] = 1 - hf
    mc[0:16, 2:18] = np.eye(16, dtype=np.float32)
    invc = np.zeros((128, 4, 16), np.float32)
    for gi, w in enumerate((2, 4, 8, 16)):
        invc[:, gi, :] = 1.0 / np.minimum(w, np.arange(16) + 1.0)
    import ml_dtypes
    ident = np.eye(128, dtype=np.float32).astype(ml_dtypes.bfloat16)
    iq = np.broadcast_to(np.eye(16, dtype=np.float32)[None], (128, 16, 16))
    f = lambda a: np.ascontiguousarray(a.reshape(128, -1), dtype=np.float32)
    return dict(cosT=cosT, sinT=sinT, dect=f(dect), qdec=f(qdec), kdt=f(kdt), kfin=f(kf), mc=mc,
                invc=f(invc), ident=ident, iq=f(iq))


_NC = None
_LAST_REC = None


def kernel(x_prompt, x_sample, state_pool, state_ret, meta_tokens, pool_norm, pool_w, pool_scale,
           ret_norm, ret_w_in, ret_w_out, mlp_norm, mlp_w_up, mlp_w_down, final_norm):
    global _NC
    A = lambda a: np.ascontiguousarray(np.asarray(a), dtype=np.float32)
    x_prompt, x_sample, state_pool, state_ret, meta_tokens = map(A, (x_prompt, x_sample, state_pool, state_ret, meta_tokens))
    vecs = np.stack([A(pool_norm)[0], A(pool_norm)[1], A(pool_scale)[0], A(pool_scale)[1], A(ret_norm)[0], A(ret_norm)[1],
                     A(mlp_norm)[0], A(mlp_norm)[1], A(mlp_norm)[2], A(mlp_norm)[3], A(final_norm)], 0)
    gv = np.ascontiguousarray(vecs.reshape(11, 16, 128).transpose(2, 0, 1).reshape(128, 11 * 16))
    shared = dict(gv=gv, pool_w=A(pool_w).reshape(2, 2048, 512), ret_w_in=A(ret_w_in), ret_w_out=A(ret_w_out),
                  mlp_w_up=A(mlp_w_up), mlp_w_down=A(mlp_w_down))
    consts = [_consts(0), _consts(1)]
    in_maps = []
    for c in range(8):
        s, hf = c // 2, c % 2
        rows = np.concatenate([x_sample[16 * c:16 * c + 16, 0, :], meta_tokens, x_prompt[s, hf * 1024:(hf + 1) * 1024, :]], 0)
        m = dict(shared)
        m.update(consts[hf])
        m["xT"] = np.ascontiguousarray(rows.T)
        sp = state_pool[:, 16 * c:16 * c + 16]
        m["poolprev"] = np.ascontiguousarray(sp)
        m["poolprevT"] = np.ascontiguousarray(sp.transpose(0, 3, 1, 2))
        m["state_ret"] = np.ascontiguousarray(state_ret[:, 16 * c:16 * c + 16])
        in_maps.append(m)
    if _NC is None:
        _NC = build_program()
    res = run_bass_kernel_spmd(_NC, in_maps, core_ids=list(range(8))).results
    y_prompt = np.zeros((4, 2048, D), np.float32)
    y_sample = np.zeros((128, 1, D), np.float32)
    npp = np.zeros((2, 4, 15, D), np.float32)
    nps = np.zeros((2, 128, 15, D), np.float32)
    nrp = np.zeros((2, 4, NH, 256, 512), np.float32)
    nrs = np.zeros((2, 128, NH, 256, 512), np.float32)
    for c in range(8):
        s, hf = c // 2, c % 2
        r = res[c]
        yt = r["yT"].T
        y_sample[16 * c:16 * c + 16, 0, :] = yt[0:16]
        y_prompt[s, hf * 1024:(hf + 1) * 1024, :] = yt[32:]
        nps[:, 16 * c:16 * c + 16, 0:14, :] = r["pool_sprev"]
        nps[:, 16 * c:16 * c + 16, 14, :] = r["pool_su"].transpose(0, 2, 1)
        nrs[:, 16 * c:16 * c + 16] = r["ret_s"]
        if hf == 1:
            npp[:, s] = r["pool_tail"].transpose(0, 2, 1)[:, 1:16, :]
            nrp[:, s] = r["ret_p"]
    return (y_prompt, y_sample, npp, nps, nrp, nrs)
```
